# Optimizing a Trainium2 kernel written in Bass

```python
import jax, jax.numpy as jnp
from jax import lax
import numpy as np

D_MODEL = 1024
BATCH = 16
SEQ = 2048
DEPTH = 1
DEC_BATCH = 128
DEC_SEQ = 8
PAST_LEN = 8192
PAGE_SIZE = 128

HEAD_DIM = 64
N_ATT_HEADS = 8
N_KV_HEADS = 4
Q_PER_KV = N_ATT_HEADS // N_KV_HEADS
D_ATT = N_ATT_HEADS * HEAD_DIM
D_KV = N_KV_HEADS * HEAD_DIM
ROT_DIM = HEAD_DIM // 4
ROPE_THETA = 500000.0
DILATED_GROUPS = ((128, 1), (512, 4), (2048, 16))
MAX_WINDOW = max(w for w, _ in DILATED_GROUPS)
ATT_BLOCK = 128
ATT_SCALE = HEAD_DIM ** -0.5
NEG_BIG = -1e30
SSM_HEAD_DIM = 64
N_SSM_HEADS = 8
D_SSM = N_SSM_HEADS * SSM_HEAD_DIM
SSM_GROUPS = 2
HEADS_PER_SSM_GROUP = N_SSM_HEADS // SSM_GROUPS
D_STATE = 128
CONV_WIDTH = 4
SSD_CHUNK = 128
D_XBC = D_SSM + 2 * SSM_GROUPS * D_STATE
D_MIX = D_ATT + D_SSM
IN_SPLITS = (D_ATT, D_ATT + D_KV, D_ATT + 2 * D_KV, D_ATT + 2 * D_KV + D_SSM,
             D_ATT + 2 * D_KV + D_SSM + D_XBC)
D_IN_PROJ = D_ATT + 2 * D_KV + D_SSM + D_XBC + N_SSM_HEADS
D_FF = 4 * D_MODEL
RMS_EPS = 1e-5

kernel_name = "hymba_dilated_ssd_decoder_step"


def rmsnorm(x, g):
    xf = x.astype(jnp.float32)
    xf = xf * lax.rsqrt(jnp.mean(xf * xf, axis=-1, keepdims=True) + RMS_EPS)
    return xf.astype(x.dtype) * g


def rope(x, pos):
    half = ROT_DIM // 2
    inv = ROPE_THETA ** (-jnp.arange(0, ROT_DIM, 2, dtype=jnp.float32) / ROT_DIM)
    ang = pos.astype(jnp.float32)[:, None] * inv[None, :]
    shp = (ang.shape[0],) + (1,) * (x.ndim - 3) + (half,)
    cos, sin = jnp.cos(ang).reshape(shp), jnp.sin(ang).reshape(shp)
    x1 = x[..., :half].astype(jnp.float32)
    x2 = x[..., half:ROT_DIM].astype(jnp.float32)
    rot = jnp.concatenate([x1 * cos - x2 * sin, x2 * cos + x1 * sin], axis=-1)
    return jnp.concatenate([rot.astype(x.dtype), x[..., ROT_DIM:]], axis=-1)


def in_proj(h, pos, norm_mix, w_in):
    u = rmsnorm(h, norm_mix) @ w_in
    q, k, v, z, xbc, dt_raw = jnp.split(u, IN_SPLITS, axis=-1)
    b, s = h.shape[:2]
    q = rope(q.reshape(b, s, N_KV_HEADS, Q_PER_KV, HEAD_DIM), pos) * ATT_SCALE
    k = rope(k.reshape(b, s, N_KV_HEADS, HEAD_DIM), pos)
    v = v.reshape(b, s, N_KV_HEADS, HEAD_DIM)
    return q, k, v, z, xbc, dt_raw


def dilated_group_prompt(q, k, v, dilation, n_keys):
    b, s = q.shape[:2]
    L = s // dilation
    nb = -(-L // ATT_BLOCK)
    Lp = nb * ATT_BLOCK

    def to_sub(t):
        rest = t.shape[2:]
        t = t.reshape((b, L, dilation) + rest)
        t = jnp.swapaxes(t, 1, 2).reshape((b * dilation, L) + rest)
        t = jnp.pad(t, ((0, 0), (0, Lp - L)) + ((0, 0),) * len(rest))
        return t.reshape((b * dilation, nb, ATT_BLOCK) + rest)

    def band(t):
        prev = jnp.pad(t[:, :-1], ((0, 0), (1, 0)) + ((0, 0),) * (t.ndim - 2))
        return jnp.concatenate([prev, t], axis=2)

    qs = to_sub(q)
    kb, vb = band(to_sub(k)), band(to_sub(v))
    scores = jnp.einsum('znikgd,znjkd->znkgij', qs, kb, preferred_element_type=jnp.float32)
    i = jnp.arange(ATT_BLOCK)[:, None]
    j = jnp.arange(2 * ATT_BLOCK)[None, :]
    dist = i + ATT_BLOCK - j
    key_sub = jnp.arange(nb)[:, None, None] * ATT_BLOCK - ATT_BLOCK + j[None]
    mask = (dist >= 0)[None] & (dist <= n_keys)[None] & (key_sub >= 0)
    scores = jnp.where(mask[None, :, None, None], scores, NEG_BIG)
    lse = jax.nn.logsumexp(scores, axis=-1)
    p = jnp.exp(scores - lse[..., None])
    o = jnp.einsum('znkgij,znjkd->znikgd', p.astype(vb.dtype), vb)
    lse = jnp.moveaxis(lse, -1, 2)

    def from_sub(t):
        rest = t.shape[3:]
        t = t.reshape((b, dilation, Lp) + rest)[:, :, :L]
        return jnp.swapaxes(t, 1, 2).reshape((b, s) + rest)

    return from_sub(o), from_sub(lse)


def dilated_group_sample(q, k_ext, v_ext, dilation, n_keys, n_past):
    t = q.shape[1]
    idx = n_past + jnp.arange(t)[:, None] - dilation * jnp.arange(n_keys + 1)[None, :]
    valid = idx >= 0
    idx = jnp.maximum(idx, 0)
    kg, vg = k_ext[:, idx], v_ext[:, idx]
    scores = jnp.einsum('btkgd,btjkd->btkgj', q, kg, preferred_element_type=jnp.float32)
    scores = jnp.where(valid[None, :, None, None, :], scores, NEG_BIG)
    lse = jax.nn.logsumexp(scores, axis=-1)
    p = jnp.exp(scores - lse[..., None])
    o = jnp.einsum('btkgj,btjkd->btkgd', p.astype(vg.dtype), vg)
    return o, lse


def combine_groups(groups):
    o = jnp.stack([g[0] for g in groups]).astype(jnp.float32)
    lse = jnp.stack([g[1] for g in groups])
    w = jax.nn.softmax(lse, axis=0)
    return jnp.sum(w[..., None] * o, axis=0)


def causal_conv(xbc, prefix, conv_w, conv_b):
    s = xbc.shape[1]
    ext = jnp.concatenate([prefix.astype(xbc.dtype), xbc], axis=1)
    y = conv_b
    for tap in range(CONV_WIDTH):
        y = y + ext[:, tap:tap + s] * conv_w[tap]
    return jax.nn.silu(y), ext[:, s:]


def ssd_chunked(x, dt, A, Bm, Cm, h0):
    f32 = jnp.float32
    x, Bm, Cm, h0 = x.astype(f32), Bm.astype(f32), Cm.astype(f32), h0.astype(f32)
    b, s, h, p = x.shape
    q = min(SSD_CHUNK, s)
    nc = -(-s // q)
    pad = nc * q - s

    def chunks(t):
        t = jnp.pad(t, ((0, 0), (0, pad)) + ((0, 0),) * (t.ndim - 2))
        return t.reshape((b, nc, q) + t.shape[2:])

    x, dt, Bm, Cm = chunks(x), chunks(dt), chunks(Bm), chunks(Cm)
    a_cum = jnp.cumsum(dt * A, axis=2)
    causal = jnp.tril(jnp.ones((q, q), dtype=bool))
    seg = a_cum[:, :, :, None, :] - a_cum[:, :, None, :, :]
    decay = jnp.exp(jnp.where(causal[:, :, None], seg, -jnp.inf))
    cb = jnp.einsum('bcihn,bcjhn->bcijh', Cm, Bm)
    y_diag = jnp.einsum('bcijh,bcjhp->bcihp', cb * decay * dt[:, :, None], x)
    to_end = jnp.exp(a_cum[:, :, -1:] - a_cum) * dt
    chunk_states = jnp.einsum('bcjhn,bcjh,bcjhp->bchpn', Bm, to_end, x)
    chunk_decay = jnp.exp(a_cum[:, :, -1])

    def step(hc, inp):
        dec, st = inp
        return dec[:, :, None, None] * hc + st, hc

    h_final, h_prev = lax.scan(step, h0, (jnp.moveaxis(chunk_decay, 1, 0),
                                          jnp.moveaxis(chunk_states, 1, 0)))
    h_prev = jnp.moveaxis(h_prev, 0, 1)
    y_off = jnp.einsum('bcihn,bchpn,bcih->bcihp', Cm, h_prev, jnp.exp(a_cum))
    y = (y_diag + y_off).reshape(b, nc * q, h, p)[:, :s]
    return y, h_final


def ssm_branch(z, xbc, dt_raw, conv_prefix, h0, conv_w, conv_b, dt_bias, a_log, d_skip, ssm_norm):
    b, s = z.shape[:2]
    xbc, conv_state = causal_conv(xbc, conv_prefix, conv_w, conv_b)
    xs = xbc[..., :D_SSM].reshape(b, s, N_SSM_HEADS, SSM_HEAD_DIM)
    Bm = xbc[..., D_SSM:D_SSM + SSM_GROUPS * D_STATE].reshape(b, s, SSM_GROUPS, D_STATE)
    Cm = xbc[..., D_SSM + SSM_GROUPS * D_STATE:].reshape(b, s, SSM_GROUPS, D_STATE)
    Bm = jnp.repeat(Bm, HEADS_PER_SSM_GROUP, axis=2)
    Cm = jnp.repeat(Cm, HEADS_PER_SSM_GROUP, axis=2)
    dt = jax.nn.softplus(dt_raw.astype(jnp.float32) + dt_bias.astype(jnp.float32))
    A = -jnp.exp(a_log.astype(jnp.float32))
    y, h_final = ssd_chunked(xs, dt, A, Bm, Cm, h0)
    y = y + d_skip.astype(jnp.float32)[:, None] * xs.astype(jnp.float32)
    y = y.reshape(b, s, D_SSM) * jax.nn.silu(z.astype(jnp.float32))
    yg = y.reshape(b, s, SSM_GROUPS, D_SSM // SSM_GROUPS)
    yg = yg * lax.rsqrt(jnp.mean(yg * yg, axis=-1, keepdims=True) + RMS_EPS)
    y = yg.reshape(b, s, D_SSM).astype(z.dtype) * ssm_norm
    return y, conv_state, h_final


def out_and_mlp(h, att, ssm, w_out, norm_mlp, w_up, w_down):
    h = h + jnp.concatenate([att, ssm], axis=-1) @ w_out
    u = jax.nn.relu(rmsnorm(h, norm_mlp) @ w_up)
    return h + (u * u) @ w_down


def layer_prompt(h, lw):
    w_in, w_out, conv_w, conv_b, dt_bias, a_log, d_skip, ssm_norm, norm_mix, norm_mlp, w_up, w_down = lw
    b, s = h.shape[:2]
    q, k, v, z, xbc, dt_raw = in_proj(h, jnp.arange(s), norm_mix, w_in)
    groups = [dilated_group_prompt(q, k, v, d, w // d) for (w, d) in DILATED_GROUPS]
    att = combine_groups(groups).reshape(b, s, D_ATT).astype(h.dtype)
    conv_prefix = jnp.zeros((b, CONV_WIDTH - 1, D_XBC), h.dtype)
    h0 = jnp.zeros((b, N_SSM_HEADS, SSM_HEAD_DIM, D_STATE), jnp.float32)
    ssm, conv_state, ssm_state = ssm_branch(z, xbc, dt_raw, conv_prefix, h0, conv_w, conv_b,
                                            dt_bias, a_log, d_skip, ssm_norm)
    h = out_and_mlp(h, att, ssm, w_out, norm_mlp, w_up, w_down)
    keep = min(MAX_WINDOW, s)
    return h, k[:, s - keep:], v[:, s - keep:], conv_state, ssm_state


def layer_sample(h, cache_k, cache_v, conv_prefix, h0, lw):
    w_in, w_out, conv_w, conv_b, dt_bias, a_log, d_skip, ssm_norm, norm_mix, norm_mlp, w_up, w_down = lw
    b, t = h.shape[:2]
    n_past = cache_k.shape[1]
    q, k, v, z, xbc, dt_raw = in_proj(h, PAST_LEN + jnp.arange(t), norm_mix, w_in)
    k_ext = jnp.concatenate([cache_k.astype(k.dtype), k], axis=1)
    v_ext = jnp.concatenate([cache_v.astype(v.dtype), v], axis=1)
    groups = [dilated_group_sample(q, k_ext, v_ext, d, w // d, n_past) for (w, d) in DILATED_GROUPS]
    att = combine_groups(groups).reshape(b, t, D_ATT).astype(h.dtype)
    ssm, conv_state, ssm_state = ssm_branch(z, xbc, dt_raw, conv_prefix, h0, conv_w, conv_b,
                                            dt_bias, a_log, d_skip, ssm_norm)
    h = out_and_mlp(h, att, ssm, w_out, norm_mlp, w_up, w_down)
    return h, k_ext[:, t:], v_ext[:, t:], conv_state, ssm_state


def setup_inputs(seed: int = 0) -> dict:
    key = jax.random.key(seed)
    ks = jax.random.split(key, 20)
    f32 = jnp.float32
    n_past = min(MAX_WINDOW, PAST_LEN)

    def nrm(k, shape, scale):
        return scale * jax.random.normal(k, shape, f32)

    dt0 = jnp.exp(jax.random.uniform(ks[10], (DEPTH, N_SSM_HEADS), f32,
                                     np.log(1e-3), np.log(1e-1)))
    return {
        "x_prompt": nrm(ks[0], (BATCH, SEQ, D_MODEL), 1.0),
        "x_sample": nrm(ks[1], (DEC_BATCH, DEC_SEQ, D_MODEL), 1.0),
        "cache_k": nrm(ks[2], (DEPTH, DEC_BATCH, n_past, N_KV_HEADS, HEAD_DIM), 1.0),
        "cache_v": nrm(ks[3], (DEPTH, DEC_BATCH, n_past, N_KV_HEADS, HEAD_DIM), 1.0),
        "state_conv": nrm(ks[4], (DEPTH, DEC_BATCH, CONV_WIDTH - 1, D_XBC), 1.0),
        "state_ssm": nrm(ks[5], (DEPTH, DEC_BATCH, N_SSM_HEADS, SSM_HEAD_DIM, D_STATE), 0.1),
        "w_in": nrm(ks[6], (DEPTH, D_MODEL, D_IN_PROJ), D_MODEL ** -0.5),
        "w_out": nrm(ks[7], (DEPTH, D_MIX, D_MODEL), D_MIX ** -0.5),
        "conv_w": nrm(ks[8], (DEPTH, CONV_WIDTH, D_XBC), CONV_WIDTH ** -0.5),
        "conv_b": nrm(ks[9], (DEPTH, D_XBC), 0.01),
        "dt_bias": dt0 + jnp.log(-jnp.expm1(-dt0)),
        "a_log": jnp.log(jax.random.uniform(ks[11], (DEPTH, N_SSM_HEADS), f32, 1.0, 16.0)),
        "d_skip": 1.0 + nrm(ks[12], (DEPTH, N_SSM_HEADS), 0.1),
        "ssm_norm": 1.0 + nrm(ks[13], (DEPTH, D_SSM), 0.01),
        "norm_mix": 1.0 + nrm(ks[14], (DEPTH, D_MODEL), 0.01),
        "norm_mlp": 1.0 + nrm(ks[15], (DEPTH, D_MODEL), 0.01),
        "w_up": nrm(ks[16], (DEPTH, D_MODEL, D_FF), D_MODEL ** -0.5),
        "w_down": nrm(ks[17], (DEPTH, D_FF, D_MODEL), D_FF ** -0.5),
        "norm_final": 1.0 + nrm(ks[18], (D_MODEL,), 0.01),
    }


def reference(x_prompt, x_sample, cache_k, cache_v, state_conv, state_ssm,
              w_in, w_out, conv_w, conv_b, dt_bias, a_log, d_skip, ssm_norm,
              norm_mix, norm_mlp, w_up, w_down, norm_final):
    hp, hs = x_prompt, x_sample
    kp, vp, cp, sp = [], [], [], []
    ksm, vsm, csm, ssm_s = [], [], [], []
    for l in range(DEPTH):
        lw = (w_in[l], w_out[l], conv_w[l], conv_b[l], dt_bias[l], a_log[l], d_skip[l],
              ssm_norm[l], norm_mix[l], norm_mlp[l], w_up[l], w_down[l])
        hp, k1, v1, c1, s1 = layer_prompt(hp, lw)
        hs, k2, v2, c2, s2 = layer_sample(hs, cache_k[l], cache_v[l], state_conv[l], state_ssm[l], lw)
        kp.append(k1); vp.append(v1); cp.append(c1); sp.append(s1)
        ksm.append(k2); vsm.append(v2); csm.append(c2); ssm_s.append(s2)
    y_prompt = rmsnorm(hp, norm_final)
    y_sample = rmsnorm(hs, norm_final)
    return (y_prompt, y_sample,
            jnp.stack(kp), jnp.stack(vp), jnp.stack(cp), jnp.stack(sp),
            jnp.stack(ksm), jnp.stack(vsm), jnp.stack(csm), jnp.stack(ssm_s))
```

```python
import contextlib
import os
import numpy as np
import ml_dtypes
import concourse.bass as bass
import concourse.mybir as mybir
from concourse.bass_utils import run_bass_kernel_spmd

F32 = mybir.dt.float32
BF16 = mybir.dt.bfloat16
AF = mybir.ActivationFunctionType
ALU = mybir.AluOpType

ENGS = ("pe", "act", "dve", "pool", "sp")
NCORES = 8
S = 2048
D = 1024
NEG = -30000.0
EPS = 1e-5


class _Op:
    __slots__ = ("eng", "fn", "deps", "seq", "gidx", "is_dma", "sem", "val", "prev_val",
                 "needs_signal", "sig_val", "waits", "cost", "seg", "start", "finish", "nsucc", "succ", "tset")


def _prod(shape):
    n = 1
    for v in shape:
        n *= int(v)
    return n


class _Rec:
    def __getattr__(self, name):
        def mk(*a, **kw):
            fn = lambda e: getattr(e, name)(*a, **kw)
            fn.meth = name
            fn.a = a
            fn.kw = kw
            return fn
        return mk


I = _Rec()


def _est_cost(eng, fn):
    name = getattr(fn, "meth", "")
    a, kw = getattr(fn, "a", ()), getattr(fn, "kw", {})
    out = kw.get("out", a[0] if a else None)
    try:
        elems = _prod(out.shape[1:])
    except Exception:
        elems = 512
    if eng == "pe":
        if name == "transpose":
            return 135.0
        lhsT = kw.get("lhsT")
        f = 3.0 if (lhsT is not None and lhsT.dtype == F32) else 1.0
        return (45.0 + 0.47 * elems) * f
    if eng == "act":
        return 200.0 + 0.45 * elems
    if eng == "dve":
        return 70.0 + 1.05 * elems
    if eng == "pool":
        return 120.0 + 2.1 * elems
    return 100.0


class Prog:
    def __init__(self, nc, n_dma_sems=20, reorder=True):
        self.nc = nc
        self.all = []
        self.last_w = {}
        self.readers = {}
        self.n_dma_sems = n_dma_sems
        self.seg = 0
        self.reorder = reorder

    def barrier(self):
        self.seg += 1

    def _record(self, o, r, w, after=()):
        deps = set(after)
        for x in r:
            lw = self.last_w.get(x)
            if lw is not None:
                deps.add(lw)
        for x in w:
            lw = self.last_w.get(x)
            if lw is not None:
                deps.add(lw)
            for rd in self.readers.get(x, ()):
                deps.add(rd)
        deps.discard(o)
        o.deps = deps
        for x in w:
            self.last_w[x] = o
            self.readers[x] = []
        for x in r:
            self.readers.setdefault(x, []).append(o)
        o.gidx = len(self.all)
        o.seg = self.seg
        o.needs_signal = False
        o.waits = []
        o.sig_val = 0
        o.sem = None; o.val = 0; o.prev_val = 0
        self.all.append(o)
        return o

    def op(self, eng, fn, r=(), w=(), c=None):
        o = _Op()
        o.eng = eng; o.fn = fn; o.is_dma = False
        o.tset = None
        if eng == "act" and getattr(fn, "meth", "") == "activation":
            f_ = fn.kw.get("func")
            if f_ in (AF.Exp, AF.Ln):
                o.tset = "exp"
            elif f_ == AF.Silu:
                o.tset = "silu"
            elif f_ == AF.Sqrt:
                o.tset = "sqrt"
        o.cost = c if c is not None else _est_cost(eng, fn)
        return self._record(o, r, w)

    def dma(self, queue, out, in_, r=(), w=(), after=(), **kw):
        o = _Op()
        o.eng = queue; o.is_dma = True
        o.tset = None
        try:
            nbytes = _prod(out.shape) * (4 if out.dtype == F32 else 2)
        except Exception:
            nbytes = 65536
        o.cost = 2500.0 + nbytes / 120.0
        o.fn = I.dma_start(out=out, in_=in_, **kw)
        return self._record(o, r, w, after)

    def _schedule_segment(self, ops, t_base):
        import heapq
        inseg = set(ops)
        for o in ops:
            o.succ = []
        for o in ops:
            n = 0
            for d in o.deps:
                if d in inseg:
                    d.succ.append(o)
                    n += 1
            o.nsucc = n
            o.start = t_base
        eng_time = {e: t_base for e in ENGS}
        fut = {e: [] for e in ENGS}
        now = {e: [] for e in ENGS}
        for o in ops:
            if o.nsucc == 0:
                heapq.heappush(fut[o.eng], (o.start, o.gidx, o))
        order = []
        act_set = [None]
        remaining = len(ops)
        while remaining:
            best = None
            for e in ENGS:
                f, n_ = fut[e], now[e]
                while f and f[0][0] <= eng_time[e]:
                    rt, gi, oo = heapq.heappop(f)
                    heapq.heappush(n_, (gi, oo))
                if n_:
                    cand = (eng_time[e], n_[0][0], e, True)
                elif f:
                    cand = (f[0][0], f[0][1], e, False)
                else:
                    continue
                if best is None or cand < best:
                    best = cand
            st_, gi, e, from_now = best
            if from_now:
                if e == "act" and len(now[e]) > 1:
                    cands = heapq.nsmallest(6, now[e])
                    pick = None
                    for cnd in cands:
                        if cnd[1].tset is None or cnd[1].tset == act_set[0]:
                            pick = cnd
                            break
                    if pick is None:
                        pick = cands[0]
                    now[e].remove(pick)
                    heapq.heapify(now[e])
                    o = pick[1]
                else:
                    _, o = heapq.heappop(now[e])
            else:
                _, _, o = heapq.heappop(fut[e])
            if e == "act" and o.tset is not None:
                if act_set[0] != o.tset:
                    st_ += 1300.0
                act_set[0] = o.tset
            o.start = st_
            if o.is_dma:
                issue = 900.0 if e == "pool" else 70.0
                o.finish = st_ + o.cost
                eng_time[e] = st_ + issue
            else:
                o.finish = st_ + o.cost
                eng_time[e] = o.finish
            order.append(o)
            remaining -= 1
            for sc in o.succ:
                lat = 0.0 if (sc.eng == e and e == "pe") else (120.0 if sc.eng == e else 260.0)
                if o.finish + lat > sc.start:
                    sc.start = o.finish + lat
                sc.nsucc -= 1
                if sc.nsucc == 0:
                    heapq.heappush(fut[sc.eng], (sc.start, sc.gidx, sc))
        t_end = max([eng_time[e] for e in ENGS] + [o.finish for o in ops])
        return order, t_end

    def emit(self):
        nc = self.nc
        segs = {}
        for o in self.all:
            segs.setdefault(o.seg, []).append(o)
        glob = []
        t = 0.0
        for k in sorted(segs):
            if self.reorder:
                t_prev = t
                order, t = self._schedule_segment(segs[k], t)
                busy = {e: sum(o.cost for o in order if o.eng == e and not o.is_dma) for e in ENGS}
                print("[sched] seg %d: %d ops (pe %d), est %.0f us; busy(us): %s" % (
                    k, len(order), sum(1 for o in order if o.eng == "pe"), (t - t_prev) / 1e3, {e: round(v / 1e3) for e, v in busy.items()}))
            else:
                order = segs[k]
            glob.append(order)
        pos = {}
        for order in glob:
            for o in order:
                pos[o] = len(pos)
        for order in glob:
            for o in order:
                for d in o.deps:
                    assert pos[d] < pos[o], ("schedule violates dependency", d.gidx, o.gidx)
        self.ops = {e: [] for e in ENGS}
        for order in glob:
            for o in order:
                o.seq = len(self.ops[o.eng])
                self.ops[o.eng].append(o)
        dma_sem_vals = {}
        rr = {e: 0 for e in ENGS}
        for e in ENGS:
            for o in self.ops[e]:
                if o.is_dma:
                    slot = rr[e] % self.n_dma_sems
                    rr[e] += 1
                    key = (e, slot)
                    prev = dma_sem_vals.get(key, 0)
                    o.sem = key; o.prev_val = prev; o.val = prev + 16
                    dma_sem_vals[key] = o.val
        waited_seq = {f: {e: -1 for e in ENGS} for f in ENGS}
        waited_dma = {f: {} for f in ENGS}
        fence_eng = None
        fence_dma = {}
        pending_fence = set()
        run_dma = {}
        last_op = {}
        for si, order in enumerate(glob):
            if si > 0:
                fence_eng = dict(last_op)
                fence_dma = dict(run_dma)
                pending_fence = set(ENGS)
            for o in order:
                f = o.eng
                rest = []
                best = {}
                if f in pending_fence:
                    pending_fence.discard(f)
                    for e, d in fence_eng.items():
                        if e == f:
                            continue
                        if waited_seq[f][e] < d.seq:
                            best[e] = d
                    for key, v in fence_dma.items():
                        if waited_dma[f].get(key, 0) < v:
                            rest.append(("dma", key, v))
                            waited_dma[f][key] = v
                if o.is_dma and o.prev_val > 0:
                    if waited_dma[f].get(o.sem, 0) < o.prev_val:
                        rest.append(("dma", o.sem, o.prev_val))
                        waited_dma[f][o.sem] = o.prev_val
                for d in sorted(o.deps, key=lambda x: x.gidx):
                    if d.is_dma:
                        if waited_dma[f].get(d.sem, 0) < d.val:
                            rest.append(("dma", d.sem, d.val))
                            waited_dma[f][d.sem] = d.val
                    else:
                        e = d.eng
                        if e == "pe" and f == "pe":
                            continue
                        if waited_seq[f][e] >= d.seq:
                            continue
                        if e not in best or best[e].seq < d.seq:
                            best[e] = d
                for e, d in best.items():
                    d.needs_signal = True
                    waited_seq[f][e] = d.seq
                    rest.append(("eng", d))
                o.waits = rest
                if o.is_dma:
                    run_dma[o.sem] = o.val
                else:
                    last_op[f] = o
        for e in ENGS:
            c = 0
            for o in self.ops[e]:
                if (not o.is_dma) and o.needs_signal:
                    c += 1
                    o.sig_val = c
        final_waits = list(dma_sem_vals.items())
        with contextlib.ExitStack() as st:
            esem = {e: st.enter_context(nc.semaphore("es_" + e)) for e in ENGS}
            dsem = {k: st.enter_context(nc.semaphore("ds_%s_%d" % k)) for k in dma_sem_vals}
            block = st.enter_context(nc.Block())

            def run(name, e):
                for o in self.ops[name]:
                    for wt in o.waits:
                        if wt[0] == "dma":
                            e.wait_ge(dsem[wt[1]], wt[2])
                        else:
                            e.wait_ge(esem[wt[1].eng], wt[1].sig_val)
                    ins = o.fn(e)
                    if o.is_dma:
                        ins.then_inc(dsem[o.sem], 16)
                    elif o.needs_signal:
                        ins.then_inc(esem[name], 1)
                if name == "sp":
                    for k, v in final_waits:
                        e.wait_ge(dsem[k], v)

            @block.sync
            def _(e):
                run("sp", e)

            @block.tensor
            def _(e):
                run("pe", e)

            @block.scalar
            def _(e):
                run("act", e)

            @block.vector
            def _(e):
                run("dve", e)

            @block.gpsimd
            def _(e):
                run("pool", e)
        self.est_ns = t
        return {e: len(self.ops[e]) for e in ENGS}


CF = {}
_o = 0
for _n, _w in (("ident", 128), ("tri", 128), ("triB", 128), ("ones", 128), ("onesB", 128),
               ("cosP", 128), ("sinP", 128), ("cosS", 8), ("sinS", 8), ("blkcol", 16)):
    CF[_n] = (_o, _w)
    _o += _w
NCF = _o
CB = {}
_o = 0
for _n, _w in (("ident", 128), ("mcur", 256), ("mprev", 256),
               ("wcache", 256), ("wnew", 256), ("blkrow", 2048), ("dmask4", 512), ("dmaskB4", 512), ("m01", 512)):
    CB[_n] = (_o, _w)
    _o += _w
NCB = _o


def make_consts():
    cf = np.zeros((128, NCF), np.float32)
    cb = np.zeros((128, NCB), np.float32)
    j = np.arange(128)[:, None]
    i = np.arange(128)[None, :]
    blk = (j // 8) == (i // 8)

    def setf(n, v):
        o, w = CF[n]
        cf[:, o:o + w] = np.asarray(v, np.float32).reshape(128, w)

    def setb(n, v):
        o, w = CB[n]
        cb[:, o:o + w] = np.asarray(v, np.float32).reshape(128, w)

    setf("ident", np.eye(128))
    setf("tri", (j <= i))
    setf("triB", (j <= i) & blk)
    setf("ones", np.ones((128, 128)))
    setf("onesB", blk)
    inv = 500000.0 ** (-np.arange(0, 16, 2, dtype=np.float32) / 16.0)
    pos = (np.arange(16)[None, :, None] * 128 + np.arange(128)[:, None, None]).astype(np.float32)
    ang = (pos * inv[None, None, :].astype(np.float32)).astype(np.float32)
    setf("cosP", np.cos(ang))
    setf("sinP", np.sin(ang))
    poss = (8192 + (np.arange(128) % 8)).astype(np.float32)[:, None]
    angs = (poss * inv[None, :].astype(np.float32)).astype(np.float32)
    setf("cosS", np.cos(angs))
    setf("sinS", np.sin(angs))
    setf("blkcol", (np.arange(128)[:, None] // 8) == np.arange(16)[None, :])

    setb("ident", np.eye(128))
    mc = np.where(j <= i, 0.0, NEG)
    mp = np.where(j >= i, 0.0, NEG)
    setb("m01", np.concatenate([(j <= i), (j <= i), (j >= i), (j >= i)], 1).astype(np.float32))
    setb("mcur", np.concatenate([mc, mc], 1))
    setb("mprev", np.concatenate([mp, mp], 1))
    setb("dmask4", np.tile(np.where(j <= i, 0.0, NEG), (1, 4)))
    setb("dmaskB4", np.tile(np.where((j <= i) & blk, 0.0, NEG), (1, 4)))
    W = np.zeros((128, 16, 2, 8), np.float32)
    for d in (1, 4, 16):
        for t in range(8):
            for jj in range(129):
                row = 2048 + t - d * jj
                if 0 <= row < 2048:
                    W[row // 16, row % 16, :, t] += 1.0
    setb("wcache", W)
    Wn = np.zeros((128, 16, 2, 8), np.float32)
    for b in range(16):
        for t in range(8):
            for tp in range(t + 1):
                dlt = t - tp
                w = 1.0 + (1.0 if dlt % 4 == 0 else 0.0) + (1.0 if dlt == 0 else 0.0)
                Wn[b * 8 + tp, b, :, t] = w
    setb("wnew", Wn)
    br = np.zeros((128, 16, 128), np.float32)
    for b in range(16):
        br[:, b, b * 8:(b + 1) * 8] = 1.0
    setb("blkrow", br)
    return cf, cb.astype(ml_dtypes.bfloat16)


def build(NP=2, NSB=16, stop_after=None, debug_out=False):
    nc = bass.Bass("TRN2", target_bir_lowering=False)
    NTOK = NP * S
    NST = NSB * 8

    def din(name, shape, dt=F32):
        return nc.dram_tensor(name, list(shape), dt, kind="ExternalInput")

    def dout(name, shape, dt=F32):
        return nc.dram_tensor(name, list(shape), dt, kind="ExternalOutput")

    x_prompt = din("x_prompt", [NTOK, D])
    x_sample = din("x_sample", [NST, D])
    cache_k = din("cache_k", [NSB, S * 256])
    cache_v = din("cache_v", [NSB, S * 256])
    state_conv = din("state_conv", [NSB * 3, 1024])
    state_ssm = din("state_ssm", [NSB * 512, 128])
    w_in = din("w_in", [D, 2568])
    w_out = din("w_out", [D, D])
    conv_w = din("conv_w", [4, 1024])
    conv_b = din("conv_b", [1, 1024])
    dt_bias = din("dt_bias", [1, 8])
    a_log = din("a_log", [1, 8])
    d_skip = din("d_skip", [1, 8])
    ssm_norm = din("ssm_norm", [1, 512])
    norm_mix = din("norm_mix", [1, D])
    norm_mlp = din("norm_mlp", [1, D])
    w_up = din("w_up", [D, 4096])
    w_down = din("w_down", [4096, D])
    norm_final = din("norm_final", [1, D])
    cf_d = din("cf", [128, NCF])
    cb_d = din("cb", [128, NCB], BF16)

    y_prompt = dout("y_prompt", [NTOK, D])
    y_sample = dout("y_sample", [NST, D])
    k_prompt = dout("k_prompt", [NTOK, 256])
    v_prompt = dout("v_prompt", [NTOK, 256])
    conv_prompt = dout("conv_prompt", [NP * 3, 1024])
    ssm_prompt = dout("ssm_prompt", [NP * 512, 128])
    k_sample = dout("k_sample", [NSB, S * 256])
    v_sample = dout("v_sample", [NSB, S * 256])
    conv_sample = dout("conv_sample", [NSB * 3, 1024])
    ssm_sample = dout("ssm_sample", [NSB * 512, 128])
    hbuf = nc.dram_tensor("hbuf", [NTOK + NST, D], F32, kind="Internal")
    dbg = dout("dbg", [128, 8 * 2048 + 16], F32) if debug_out else None

    P = Prog(nc, reorder=not os.environ.get("NOREORDER"))
    st = contextlib.ExitStack()

    def T(name, shape, dt):
        h = st.enter_context(nc.sbuf_tensor("sb_" + name, list(shape), dt))
        return h[:]

    def sap(t, p0, npart, off, dims):
        F = t.ap[0][0]
        return bass.AP(t.tensor, t.offset + p0 * F + off, [[F, npart]] + [list(d) for d in dims])

    ARENA_BYTES = 168 * 1024
    arena_h = st.enter_context(nc.sbuf_tensor("sb_arena", [128, ARENA_BYTES // 2], BF16))
    arena_f = arena_h.bitcast(F32)
    arena_pos = [0]

    arena_mark = [0]

    def arena_reset():
        arena_pos[0] = arena_mark[0]

    def AT(shape, dt):
        n = 1
        for v in shape[1:]:
            n *= v
        esz = 4 if dt == F32 else 2
        nbytes = (n * esz + 31) // 32 * 32
        b0 = arena_pos[0]
        arena_pos[0] += nbytes
        assert arena_pos[0] <= ARENA_BYTES, ("arena overflow", arena_pos[0])
        h = arena_f if dt == F32 else arena_h
        v = h[:, b0 // esz:b0 // esz + n]
        if len(shape) == 3:
            v = v.rearrange("p (a b) -> p a b", a=shape[1])
        elif len(shape) == 4:
            v = v.rearrange("p (a b c) -> p a b c", a=shape[1], b=shape[2])
        elif len(shape) == 5:
            v = v.rearrange("p (a b c d) -> p a b c d", a=shape[1], b=shape[2], c=shape[3])
        return v

    def dap(t, off, dims):
        return bass.AP(t, off, [list(d) for d in dims])

    ps_h = [st.enter_context(nc.psum_tensor("ps%d" % k, [128, 512], F32)) for k in range(8)]
    ps = [p[:] for p in ps_h]
    psb = [p.bitcast(BF16)[:] for p in ps_h]

    cf = T("cf", [128, NCF], F32)
    cb = T("cb", [128, NCB], BF16)
    P.dma("sp", cf[:], cf_d.ap(), w=["cf"])
    P.dma("sp", cb[:], cb_d.ap(), w=["cb"])

    def CFs(n, p0=0, npart=128):
        o, w = CF[n]
        return cf[p0:p0 + npart, o:o + w]

    def CBs(n, p0=0, npart=128):
        o, w = CB[n]
        return cb[p0:p0 + npart, o:o + w]

    gmix = T("gmix", [128, D], F32)
    P.dma("sp", gmix[:], dap(norm_mix, 0, [[0, 128], [1, D]]), w=["gmix"])
    epsb = T("epsb", [128, 1], F32)
    P.op("pool", I.memset(epsb[:], EPS), w=["epsb"])
    oneb = T("oneb", [128, 1], F32)
    P.op("pool", I.memset(oneb[:], 1.0), w=["oneb"])
    cw = T("cw", [128, 4, 8], F32)
    cbias = T("cbias", [128, 8], F32)
    dtb = T("dtb", [128, 8], F32)
    aneg = T("aneg", [128, 8], F32)
    dsk = T("dsk", [128, 8], F32)
    ssn = T("ssn", [128, 512], F32)
    for tap in range(4):
        P.dma("act", cw[:, tap, :], dap(conv_w, tap * 1024, [[1, 128], [128, 8]]), w=["cw"], allow_slow_non_contiguous=True)
    P.dma("act", cbias[:], dap(conv_b, 0, [[1, 128], [128, 8]]), w=["cbias"], allow_slow_non_contiguous=True)
    P.dma("sp", dtb[:], dap(dt_bias, 0, [[0, 128], [1, 8]]), w=["dtb"])
    P.dma("sp", aneg[:], dap(a_log, 0, [[0, 128], [1, 8]]), w=["aneg"])
    P.dma("sp", dsk[:], dap(d_skip, 0, [[0, 128], [1, 8]]), w=["dsk"])
    P.dma("sp", ssn[:], dap(ssm_norm, 0, [[0, 128], [1, 512]]), w=["ssn"])
    P.op("act", I.activation(out=aneg[:], in_=aneg[:], func=AF.Exp), r=["aneg"], w=["aneg"])
    P.op("dve", I.tensor_scalar(out=aneg[:], in0=aneg[:], scalar1=-1.0, scalar2=None, op0=ALU.mult), r=["aneg"], w=["aneg"])
    xres = [T("xres0", [128, D], F32)] * 2

    win = AT([128, 8, 2568], BF16)
    wout = AT([128, 8, D], BF16)
    mixT = AT([128, 4, S], BF16)
    arena_mark[0] = arena_pos[0]
    for k in range(8):
        P.dma("pool", win[:, k, 0:1024], w_in.ap()[k * 128:(k + 1) * 128, 0:1024], w=["winA%d" % k])
    for k in range(8):
        P.dma("pool", win[:, k, 1024:2568], w_in.ap()[k * 128:(k + 1) * 128, 1024:2568], w=["winB%d" % k])
    for k in range(8):
        P.dma("pool", wout[:, k, :], w_out.ap()[k * 128:(k + 1) * 128, :], w=["wout%d" % k])
    WINA = ["winA%d" % k for k in range(8)]
    WIN = ["winB%d" % k for k in range(8)]
    WOUT = ["wout%d" % k for k in range(8)]


    copy_jobs = []
    if not os.environ.get("NOCOPY"):
        for b in range(NSB):
            copy_jobs.append((k_sample, cache_k, b))
            copy_jobs.append((v_sample, cache_v, b))

    def cache_copy_some(n, after=()):
        for _ in range(n):
            if copy_jobs:
                dst, src, b = copy_jobs.pop(0)
                P.dma("act", dst.ap()[b:b + 1, 0:2040 * 256], src.ap()[b:b + 1, 8 * 256:S * 256], after=after)

    xs = [T("xs%d" % k, [128, D], F32) for k in range(2)]
    xn = [T("xn%d" % k, [128, D], BF16) for k in range(2)]
    ssq = [T("ssq%d" % k, [128, 4], F32) for k in range(2)]
    xT = [T("xT%d" % k, [128, 8, 128], BF16) for k in range(2)]
    def alloc_pass1():
        arena_reset()
        d_ = {}
        d_["qT"] = AT([128, 4, S], BF16)
        d_["kT"] = AT([128, 2, S], BF16)
        d_["V3"] = AT([128, 3, 16, 4, 65], BF16)
        d_["acc"] = AT([128, 2, S], F32)
        d_["ET"] = [AT([128, 512], BF16) for k in range(3)]
        d_["qk"] = [AT([128, 12, 64], F32) for k in range(2)]
        d_["rt"] = [AT([128, 4, 12, 8], F32) for k in range(2)]
        d_["vst"] = [AT([128, 256], F32) for k in range(2)]
        d_["qkb"] = [AT([128, 768], BF16) for k in range(2)]
        return d_
    A1 = alloc_pass1()
    qT, kT, V3, acc, ET, qk, rt, vst, qkb = (A1[k] for k in ("qT", "kT", "V3", "acc", "ET", "qk", "rt", "vst", "qkb"))
    V3ALL = ["V3_%d_%d" % (a, b) for a in range(3) for b in range(16)]

    def rstd_ops(sq, sqn, scale, c0=0, n=1):
        P.op("act", I.activation(out=sq[:, n + c0:2 * n + c0], in_=sq[:, c0:c0 + n], func=AF.Ln, scale=scale, bias=epsb[:, 0:1]),
             r=[sqn, "epsb"], w=[sqn])
        P.op("act", I.activation(out=sq[:, 2 * n + c0:3 * n + c0], in_=sq[:, n + c0:2 * n + c0], func=AF.Exp, scale=-0.5),
             r=[sqn], w=[sqn])

    def rmsnorm_to_xT(src_ap_rows, sl, gain, tagx, dst3=None, dst_tag=None, pb=None):
        P.dma("sp", xs[sl][:], src_ap_rows, w=["xs%d" % sl])
        P.op("pool", I.memset(ssq[sl][:], 0.0), w=["ssq%d" % sl])
        P.op("act", I.activation(out=xn[sl][:], in_=xs[sl][:], func=AF.Square,
                                           accum_out=ssq[sl][:, 0:1]),
             r=["xs%d" % sl, "ssq%d" % sl], w=["xn%d" % sl, "ssq%d" % sl])
        rstd_ops(ssq[sl], "ssq%d" % sl, 1.0 / D)
        P.op("dve", I.scalar_tensor_tensor(out=xn[sl][:], in0=xs[sl][:], scalar=ssq[sl][:, 2:3],
                                                     in1=gain[:], op0=ALU.mult, op1=ALU.mult),
             r=["xs%d" % sl, "ssq%d" % sl, tagx], w=["xn%d" % sl])
        if pb is None:
            pb = 4 * sl
        if dst3 is None:
            dst3, dst_tag = xT[sl], "xT%d" % sl
        for k in range(8):
            P.op("pe", I.transpose(out=psb[pb][:, k * 128:(k + 1) * 128],
                                                  in_=xn[sl][:, k * 128:(k + 1) * 128], identity=CBs("ident")),
                 r=["xn%d" % sl, "cb"], w=["ps%d" % pb])
        return P.op("act", I.copy(out=dst3, in_=psb[pb][:, 0:1024].rearrange("p (k t) -> p k t", k=8)),
                    r=["ps%d" % pb], w=[dst_tag])


    def alloc_pass2(sample=False):
        arena_reset()
        d_ = {}
        if not sample:
            d_["xT5"] = [AT([128, 8, 512], BF16) for k in range(2)]
        d_["pre"] = [AT([128, 520], F32) for k in range(2)]
        d_["halo"] = AT([128, 8, 4], F32)
        d_["cacc"] = [AT([128, 512], F32) for k in range(2)]
        d_["xbcT"] = [AT([128, 8, 512], BF16) for k in range(1 if sample else 2)]
        d_["tp"] = 0
        d_["tm"] = AT([128, 4, 768], BF16)
        nb_ = 1 if sample else 2
        d_["nbuf"] = nb_
        d_["sm"] = [AT([128, 64], F32) for k in range(nb_)]
        d_["dtt"] = [AT([128, 8], F32) for k in range(nb_)]
        d_["aa"] = [AT([128, 8], F32) for k in range(nb_)]
        d_["abc"] = [AT([128, 8, 128], F32) for k in range(nb_)]
        d_["decT"] = [AT([128, 8, 128], F32) for k in range(nb_)]
        d_["MT"] = [AT([128, 8, 128], BF16) for k in range(nb_)]
        d_["y0"] = [AT([128, 512], F32) for k in range(nb_)]
        d_["y1"] = [AT([128, 512], F32) for k in range(nb_)]
        d_["y2"] = [AT([128, 512], F32)]
        d_["siluz"] = [AT([128, 512], F32)]
        d_["ygn"] = [AT([128, 512], BF16) for k in range(nb_)]
        d_["xw"] = [AT([128, 512], BF16) for k in range(nb_)]
        d_["hT"] = AT([128, 512], F32)
        d_["hTb"] = AT([128, 512], BF16)
        d_["hout"] = [AT([128, D], F32)]
        d_["xtok"] = d_["decT"][-1].rearrange("p a b -> p (a b)")
        d_["ss2"] = [AT([128, 8], F32) for k in range(nb_)]
        d_["stT"] = d_["abc"][-1][:, 0:4, :]
        d_["mixS"] = [AT([128, 4, 128], BF16) for k in range(2)]
        return d_

    def bc8(ap8, n):
        return bass.AP(ap8.tensor, ap8.offset, [list(ap8.ap[0]), [1, 8], [0, n]])

    def ssd_chunk(A, sub, tokrow0, mix_cols, first_chunk, tri_n, ones_n, dmask_n, xsrc_rows, hb_row0, prompt, yoff_emit=None):
        bi = sub % A["nbuf"]
        tp_ = A["tp"]
        xT5 = A["xT5"][tp_] if isinstance(A["xT5"], list) else A["xT5"]
        xbcT = A["xbcT"][tp_]
        tm, hT, hTb = (A[k] for k in ("tm", "hT", "hTb"))
        sm, dtt, aa, abc, decT, MT, y0, y1, y2, siluz, ygn, xw, ss2 = (A[k][bi % len(A[k])] for k in (
            "sm", "dtt", "aa", "abc", "decT", "MT", "y0", "y1", "y2", "siluz", "ygn", "xw", "ss2"))
        c0, c1 = sub * 128, (sub + 1) * 128
        XT = ["xT5_%d_%d" % (tp_, sub)]
        XB = ["xbcT%d_%d_%d" % (tp_, ch, sub) for ch in range(8)]
        for ch in range(6):
            P.op("pe", I.transpose(out=psb[2][:, ch * 128:(ch + 1) * 128], in_=xbcT[:, ch, c0:c1], identity=CBs("ident")),
                 r=[XB[ch], "cb"], w=["ps2"])
        P.op("act", I.copy(out=tm[:, sub, :], in_=psb[2][:, 0:768]), r=["ps2"], w=["tm%d" % sub])
        TM = ["tm%d" % sub]
        for k in range(8):
            P.op("pe", I.matmul(ps[6][:, 256:264], lhsT=xT5[:, k, c0:c1], rhs=win[:, k, 2560:2568], start=(k == 0), stop=(k == 7)),
                 r=XT + [WIN[k]], w=["ps6"])
        P.op("dve", I.tensor_tensor(out=sm[:, 0:8], in0=ps[6][:, 256:264], in1=dtb[:], op=ALU.add), r=["ps6", "dtb"], w=["sm_%d" % bi])
        P.op("act", I.activation(out=sm[:, 8:16], in_=sm[:, 0:8], func=AF.Exp), r=["sm_%d" % bi], w=["sm_%d" % bi])
        P.op("act", I.activation(out=dtt[:], in_=sm[:, 8:16], func=AF.Ln, bias=oneb[:, 0:1]), r=["sm_%d" % bi, "oneb"], w=["dtt_%d" % bi])
        P.op("dve", I.tensor_tensor(out=aa[:], in0=dtt[:], in1=aneg[:], op=ALU.mult), r=["dtt_%d" % bi, "aneg"], w=["aa_%d" % bi])
        P.op("pe", I.matmul(ps[6][:, 264:272], lhsT=CFs(tri_n), rhs=aa[:], start=True, stop=True), r=["aa_%d" % bi, "cf"], w=["ps6"])
        P.op("pe", I.matmul(ps[6][:, 272:280], lhsT=CFs(ones_n), rhs=aa[:], start=True, stop=True), r=["aa_%d" % bi, "cf"], w=["ps6"])
        P.op("dve", I.tensor_scalar(out=sm[:, 16:24], in0=ps[6][:, 264:272], scalar1=-1.0, scalar2=None, op0=ALU.mult),
             r=["ps6"], w=["sm_%d" % bi])
        P.op("act", I.activation(out=sm[:, 24:32], in_=ps[6][:, 264:272], func=AF.Exp), r=["ps6"], w=["sm_%d" % bi])
        P.op("dve", I.tensor_tensor(out=sm[:, 32:40], in0=ps[6][:, 272:280], in1=sm[:, 16:24], op=ALU.add),
             r=["ps6", "sm_%d" % bi], w=["sm_%d" % bi])
        P.op("act", I.activation(out=sm[:, 40:48], in_=sm[:, 32:40], func=AF.Exp), r=["sm_%d" % bi], w=["sm_%d" % bi])
        P.op("dve", I.tensor_tensor(out=sm[:, 48:56], in0=sm[:, 40:48], in1=dtt[:], op=ALU.mult), r=["sm_%d" % bi, "dtt_%d" % bi], w=["sm_%d" % bi])
        P.op("act", I.activation(out=sm[:, 56:64], in_=ps[6][:, 272:280], func=AF.Exp), r=["ps6"], w=["sm_%d" % bi])
        o_t, _w = CF[tri_n]
        tri_b = bass.AP(cf.tensor, cf.offset + o_t, [list(cf.ap[0]), [0, 8], [1, 128]])
        P.op("dve", I.tensor_tensor(out=abc[:], in0=tri_b, in1=bc8(aa, 128), op=ALU.mult), r=["aa_%d" % bi, "cf"], w=["abc_%d" % bi])
        dm4 = "dmask4" if dmask_n == "dmaskF" else "dmaskB4"
        for hb in range(2):
            bank = 4 + hb
            P.op("pe", I.matmul(ps[bank][:, 0:512], lhsT=CFs("ones"), rhs=abc[:, hb * 4:(hb + 1) * 4, :].rearrange("p a b -> p (a b)"),
                                start=True, stop=False), r=["abc_%d" % bi, "cf"], w=["ps%d" % bank])
            P.op("pe", I.matmul(ps[bank][:, 0:512], lhsT=CBs("ident"), rhs=CBs(dm4), start=False, stop=True),
                 r=["cb"], w=["ps%d" % bank])
        for h in range(8):
            bank = 4 + h // 4
            col = (h % 4) * 128
            P.op("act", I.activation(out=decT[:, h, :], in_=ps[bank][:, col:col + 128], func=AF.Exp,
                                     bias=sm[:, 16 + h:17 + h]), r=["ps%d" % bank, "sm_%d" % bi], w=["decT_%d" % bi])
        for g in range(2):
            P.op("pe", I.matmul(ps[6][:, g * 128:(g + 1) * 128], lhsT=xbcT[:, 4 + g, c0:c1], rhs=xbcT[:, 6 + g, c0:c1],
                                start=True, stop=True), r=[XB[4 + g], XB[6 + g]], w=["ps6"])
        for h in range(8):
            g = h // 4
            P.op("dve", I.scalar_tensor_tensor(out=MT[:, h, :], in0=decT[:, h, :], scalar=dtt[:, h:h + 1],
                                               in1=ps[6][:, g * 128:(g + 1) * 128], op0=ALU.mult, op1=ALU.mult),
                 r=["decT_%d" % bi, "dtt_%d" % bi, "ps6"], w=["MT_%d" % bi])
        for h in range(8):
            P.op("pe", I.matmul(ps[7][:, h * 64:(h + 1) * 64], lhsT=MT[:, h, :], rhs=tm[:, sub, h * 64:(h + 1) * 64],
                                start=True, stop=True), r=["MT_%d" % bi] + TM, w=["ps7"])
        if yoff_emit is None:
            for g in range(2):
                P.op("pe", I.matmul(ps[0][:, g * 256:(g + 1) * 256], lhsT=xbcT[:, 6 + g, c0:c1], rhs=hTb[:, g * 256:(g + 1) * 256],
                                    start=True, stop=True), r=[XB[6 + g], "hTb"], w=["ps0"])
        else:
            P.op("pool", I.tensor_tensor(out=xw[:].rearrange("p (h d) -> p h d", h=8), in0=tm[:, sub, 0:512].rearrange("p (h d) -> p h d", h=8),
                                         in1=bc8(sm[:, 48:56], 64), op=ALU.mult), r=TM + ["sm_%d" % bi], w=["xw_%d" % bi])
            yoff_emit()
        v3 = lambda ap: ap.rearrange("p (h d) -> p h d", h=8)
        ysrc = [(ps[0][:, 0:256], "ps0"), (ps[0][:, 256:512], "ps0")] if yoff_emit is None else \
               [(ps[0][:, 0:256], "ps0"), (ps[6][:, 256:512], "ps6")]
        for g in range(2):
            ein = bass.AP(sm.tensor, sm.offset + 24 + 4 * g, [list(sm.ap[0]), [1, 4], [0, 64]])
            P.op("dve", I.tensor_tensor(out=y0[:, g * 256:(g + 1) * 256].rearrange("p (h d) -> p h d", h=4),
                                        in0=ysrc[g][0].rearrange("p (h d) -> p h d", h=4), in1=ein, op=ALU.mult),
                 r=[ysrc[g][1], "sm_%d" % bi], w=["y0_%d" % bi])
        P.op("pool", I.tensor_tensor(out=v3(y1[:]), in0=v3(tm[:, sub, 0:512]), in1=bc8(dsk[:], 64), op=ALU.mult),
             r=TM + ["dsk"], w=["y1_%d" % bi])
        P.op("pool", I.tensor_tensor(out=y1[:], in0=y1[:], in1=y0[:], op=ALU.add), r=["y0_%d" % bi, "y1_%d" % bi], w=["y1_%d" % bi])
        P.op("dve", I.tensor_tensor(out=y2[:], in0=ps[7][:, 0:512], in1=y1[:], op=ALU.add), r=["ps7", "y1_%d" % bi], w=["y2_0"])
        ssd_gate_and_out(A, sub, tokrow0, xsrc_rows, hb_row0)
        if not prompt:
            return
        P.op("pool", I.tensor_tensor(out=v3(xw[:]), in0=v3(tm[:, sub, 0:512]), in1=bc8(sm[:, 48:56], 64), op=ALU.mult),
             r=TM + ["sm_%d" % bi], w=["xw_%d" % bi])
        P.op("pe", I.matmul(ps[1][:, 0:256], lhsT=tm[:, sub, 512:640], rhs=xw[:, 0:256], start=True, stop=True),
             r=TM + ["xw_%d" % bi], w=["ps1"])
        P.op("pe", I.matmul(ps[1][:, 256:512], lhsT=tm[:, sub, 640:768], rhs=xw[:, 256:512], start=True, stop=True),
             r=TM + ["xw_%d" % bi], w=["ps1"])
        P.op("dve", I.tensor_tensor(out=v3(hT[:]), in0=v3(hT[:]), in1=bc8(sm[:, 56:64], 64), op=ALU.mult),
             r=["hT", "sm_%d" % bi], w=["hT"])
        P.op("dve", I.tensor_tensor(out=hT[:], in0=hT[:], in1=ps[1][:, 0:512], op=ALU.add), r=["hT", "ps1"], w=["hT"])
        P.op("act", I.copy(out=hTb[:], in_=hT[:]), r=["hT"], w=["hTb"])

    def ssd_gate_and_out(A, sub, tokrow0, xsrc_rows, hb_row0):
        bi = sub % A["nbuf"]
        tp_ = A["tp"]
        xT5 = A["xT5"][tp_] if isinstance(A["xT5"], list) else A["xT5"]
        hout = A["hout"]
        y0, y1, y2, siluz, ygn, ss2 = (A[k][bi % len(A[k])] for k in ("y0", "y1", "y2", "siluz", "ygn", "ss2"))
        c0, c1 = sub * 128, (sub + 1) * 128
        XT = ["xT5_%d" % sub]
        for k in range(8):
            P.op("pe", I.matmul(ps[1][:, 0:512], lhsT=xT5[:, k, c0:c1], rhs=win[:, k, 1024:1536], start=(k == 0), stop=(k == 7)),
                 r=XT + [WIN[k]], w=["ps1"])
        P.op("act", I.activation(out=siluz[:], in_=ps[1][:, 0:512], func=AF.Silu), r=["ps1"], w=["siluz_0"])
        P.op("dve", I.tensor_tensor(out=y0[:], in0=y2[:], in1=siluz[:], op=ALU.mult), r=["y2_0", "siluz_0"], w=["y0_%d" % bi])
        P.op("pool", I.memset(ss2[:], 0.0), w=["ss2_%d" % bi])
        for g in range(2):
            P.op("act", I.activation(out=y1[:, g * 256:(g + 1) * 256], in_=y0[:, g * 256:(g + 1) * 256], func=AF.Square,
                                     accum_out=ss2[:, g:g + 1]), r=["y0_%d" % bi, "ss2_%d" % bi], w=["y1_%d" % bi, "ss2_%d" % bi])
        rstd_ops(ss2, "ss2_%d" % bi, 1.0 / 256, 0, 2)
        for g in range(2):
            P.op("dve", I.scalar_tensor_tensor(out=ygn[:, g * 256:(g + 1) * 256], in0=y0[:, g * 256:(g + 1) * 256],
                                               scalar=ss2[:, 4 + g:5 + g], in1=ssn[:, g * 256:(g + 1) * 256],
                                               op0=ALU.mult, op1=ALU.mult), r=["y0_%d" % bi, "ss2_%d" % bi, "ssn"], w=["ygn_%d" % bi])
        YB = 3
        for k in range(4):
            P.op("pe", I.transpose(out=psb[YB][:, k * 128:(k + 1) * 128], in_=ygn[:, k * 128:(k + 1) * 128], identity=CBs("ident")),
                 r=["ygn_%d" % bi, "cb"], w=["ps%d" % YB])
        sl = sub % 2
        hs_ = 0
        mixS = A["mixS"][sl]
        P.op("act", I.copy(out=mixS[:], in_=psb[YB][:, 0:512].rearrange("p (k t) -> p k t", k=4)),
             r=["ps%d" % YB], w=["mixS%d" % sl])
        P.dma("sp", xres[sl][:], xsrc_rows, w=["xres"])
        for half in range(2):
            ob = (7, 0)[half]
            for k in range(8):
                lh = mixT[:, k, tokrow0:tokrow0 + 128] if k < 4 else mixS[:, k - 4, :]
                P.op("pe", I.matmul(ps[ob][:, 0:512], lhsT=lh,
                                    rhs=wout[:, k, half * 512:(half + 1) * 512], start=(k == 0), stop=(k == 7)),
                     r=["mixA", "mixS%d" % sl, WOUT[k]], w=["ps%d" % ob])
            P.op("dve", I.tensor_tensor(out=hout[hs_][:, half * 512:(half + 1) * 512], in0=ps[ob][:, 0:512],
                                        in1=xres[sl][:, half * 512:(half + 1) * 512], op=ALU.add),
                 r=["ps%d" % ob, "xres"], w=["hout%d" % hs_])
        P.dma("sp", hbuf.ap()[hb_row0:hb_row0 + 128, :], hout[hs_][:], r=["hout%d" % hs_], w=["hbuf_%d" % hb_row0])

    def ssd_pass(A, prompt, s):
        pre, halo, cacc, hT, hTb, xtok, stT = (A[k] for k in ("pre", "halo", "cacc", "hT", "hTb", "xtok", "stT"))
        P.op("pool", I.memset(halo[:].rearrange("p a b -> p (a b)"), 0.0), w=["halo"])
        P.op("pool", I.memset(hT[:], 0.0), w=["hT"])
        P.op("pool", I.memset(hTb[:], 0.0), w=["hTb"])
        for Tt in range(4):
            tb = s * S + Tt * 512
            tp_ = Tt % 2
            A["tp"] = tp_
            xT5, xbcT = A["xT5"][tp_], A["xbcT"][tp_]
            for sub in range(4):
                rmsnorm_to_xT(x_prompt.ap()[tb + sub * 128:tb + (sub + 1) * 128, :], sub % 2, gmix, "gmix",
                              dst3=xT5[:, :, sub * 128:(sub + 1) * 128], dst_tag="xT5_%d_%d" % (tp_, sub), pb=2)
            XTall = ["xT5_%d_%d" % (tp_, x) for x in range(4)]
            XO = 0
            for ch in range(8):
                u = ch % 2
                for k in range(8):
                    P.op("pe", I.matmul(ps[XO + u][:, 0:512], lhsT=win[:, k, 1536 + ch * 128:1536 + (ch + 1) * 128],
                                        rhs=xT5[:, k, :], start=(k == 0), stop=(k == 7)), r=XTall + [WIN[k]], w=["ps%d" % (XO + u)])
                P.op("act", I.copy(out=pre[u][:, 3:515], in_=ps[XO + u][:, 0:512]), r=["ps%d" % (XO + u)], w=["pre%d" % u])
                P.op("pool", I.tensor_copy(out=pre[u][:, 0:3], in_=halo[:, ch, 0:3]), r=["halo"], w=["pre%d" % u])
                P.op("dve", I.tensor_scalar(out=cacc[u][:], in0=pre[u][:, 3:515], scalar1=cw[:, 3, ch:ch + 1], scalar2=cbias[:, ch:ch + 1],
                                            op0=ALU.mult, op1=ALU.add), r=["pre%d" % u, "cw", "cbias"], w=["cacc%d" % u])
                for tap in (2, 1, 0):
                    P.op("dve", I.scalar_tensor_tensor(out=cacc[u][:], in0=pre[u][:, tap:tap + 512], scalar=cw[:, tap, ch:ch + 1],
                                                       in1=cacc[u][:], op0=ALU.mult, op1=ALU.add),
                         r=["pre%d" % u, "cw", "cacc%d" % u], w=["cacc%d" % u])
                P.op("act", I.activation(out=xbcT[:, ch, :], in_=cacc[u][:], func=AF.Silu), r=["cacc%d" % u],
                     w=["xbcT%d_%d_%d" % (tp_, ch, x) for x in range(4)])
                P.op("pool", I.tensor_copy(out=halo[:, ch, 0:3], in_=pre[u][:, 512:515]), r=["pre%d" % u], w=["halo"])
            for sub in range(4):
                tr = Tt * 512 + sub * 128
                ssd_chunk(A, sub, tr, None, (Tt == 0 and sub == 0), "tri", "ones", "dmaskF",
                          x_prompt.ap()[tb + sub * 128:tb + (sub + 1) * 128, :], tb + sub * 128, True)
            if Tt == 3:
                for half in range(2):
                    for k in range(8):
                        P.op("pe", I.matmul(ps[4 + half][:, 0:512], lhsT=xT5[:, k, 384:512],
                                            rhs=win[:, k, 1536 + half * 512:1536 + (half + 1) * 512], start=(k == 0), stop=(k == 7)),
                             r=["xT5_%d_3" % tp_, WIN[k]], w=["ps%d" % (4 + half)])
                    P.op("act", I.copy(out=xtok[:, half * 512:(half + 1) * 512], in_=ps[4 + half][:, 0:512]),
                         r=["ps%d" % (4 + half)], w=["decT_1"])
                P.dma("sp", conv_prompt.ap()[s * 3:s * 3 + 3, :], xtok[125:128, :], r=["decT_1"])
        for q in range(4):
            P.op("pe", I.transpose(out=ps[7][:, q * 128:(q + 1) * 128], in_=hT[:, q * 128:(q + 1) * 128], identity=CFs("ident")),
                 r=["hT", "cf"], w=["ps7"])
        P.op("act", I.copy(out=stT[:].rearrange("p q n -> p (q n)"), in_=ps[7][:, 0:512]), r=["ps7"], w=["abc_1"])
        P.dma("sp", dap(ssm_prompt, s * 512 * 128, [[128, 128], [128 * 128, 4], [1, 128]]), stT[:], r=["abc_1"])

    def rope_qk(sl, cos_ap, sin_ap, pq, pkv):
        P.op("act", I.mul(out=qk[sl][:, 0:8, :].rearrange("p h d -> p (h d)"), in_=ps[pq][:, 0:512],
                                    mul=0.125), r=["ps%d" % pq], w=["qk%d" % sl])
        P.op("act", I.copy(out=qk[sl][:, 8:12, :].rearrange("p h d -> p (h d)"), in_=ps[pkv][:, 0:256]),
             r=["ps%d" % pkv], w=["qk%d" % sl])
        x1 = qk[sl][:, :, 0:8]
        x2 = qk[sl][:, :, 8:16]
        def bc(ap2):
            return bass.AP(ap2.tensor, ap2.offset, [list(ap2.ap[0]), [0, 12], [1, 8]])
        cb_, sb_ = bc(cos_ap), bc(sin_ap)
        R = ["qk%d" % sl, "cf"]
        P.op("dve", I.tensor_tensor(out=rt[sl][:, 0], in0=x1, in1=cb_, op=ALU.mult), r=R, w=["rt%d" % sl])
        P.op("dve", I.tensor_tensor(out=rt[sl][:, 1], in0=x2, in1=sb_, op=ALU.mult), r=R, w=["rt%d" % sl])
        P.op("dve", I.tensor_tensor(out=rt[sl][:, 2], in0=x2, in1=cb_, op=ALU.mult), r=R, w=["rt%d" % sl])
        P.op("dve", I.tensor_tensor(out=rt[sl][:, 3], in0=x1, in1=sb_, op=ALU.mult), r=R, w=["rt%d" % sl])
        P.op("dve", I.tensor_tensor(out=x1, in0=rt[sl][:, 0], in1=rt[sl][:, 1], op=ALU.subtract),
             r=["rt%d" % sl], w=["qk%d" % sl])
        P.op("dve", I.tensor_tensor(out=x2, in0=rt[sl][:, 2], in1=rt[sl][:, 3], op=ALU.add),
             r=["rt%d" % sl], w=["qk%d" % sl])

    def qk_to_bf16(sl):
        for c in range(2):
            src = sap(qk[sl], 0, 128, c * 256, [[64, 2], [128, 2], [1, 64]])
            dst = qkb[sl][:, c * 256:(c + 1) * 256].rearrange("p (g k d) -> p g k d", g=2, k=2)
            P.op("pool", I.tensor_copy(out=dst, in_=src),
                 r=["qk%d" % sl], w=["qkb%d" % sl])
        P.op("pool", I.tensor_copy(out=qkb[sl][:, 512:768],
                                             in_=qk[sl][:, 8:12, :].rearrange("p h d -> p (h d)")),
             r=["qk%d" % sl], w=["qkb%d" % sl])

    for s in range(NP if stop_after != "sample" else 0):
        P.op("pool", I.memset(V3[:].rearrange("p a b c d -> p (a b c d)"), 1.0), w=V3ALL)
        STEP = int(os.environ.get("STEP", "99"))
        for i in range(int(os.environ.get("LIMI", "16"))):
            sl = i % 2
            pb = 4 * sl
            tok0 = s * S + i * 128
            last_ = rmsnorm_to_xT(x_prompt.ap()[tok0:tok0 + 128, :], sl, gmix, "gmix")
            cache_copy_some((2 * NSB + 16 * NP - 1) // (16 * NP), after=[last_])
            if STEP < 2:
                continue
            for half, bank in ((0, pb + 1), (1, pb + 2)):
                for k in range(8):
                    P.op("pe", I.matmul(
                        ps[bank][:, 0:512], lhsT=xT[sl][:, k, :], rhs=win[:, k, half * 512:(half + 1) * 512],
                        start=(k == 0), stop=(k == 7)),
                        r=["xT%d" % sl, WINA[k]], w=["ps%d" % bank])
            if STEP < 3:
                continue
            o, _w = CF["cosP"]
            o2, _w = CF["sinP"]
            rope_qk(sl, cf[:, o + i * 8:o + i * 8 + 8], cf[:, o2 + i * 8:o2 + i * 8 + 8], pb + 1, pb + 2)
            if STEP < 4:
                continue
            P.op("act", I.copy(out=vst[sl][:], in_=ps[pb + 2][:, 256:512]),
                 r=["ps%d" % (pb + 2)], w=["vst%d" % sl])
            P.dma("sp", k_prompt.ap()[tok0:tok0 + 128, :], qk[sl][:, 8:12, :].rearrange("p h d -> p (h d)"),
                  r=["qk%d" % sl])
            P.dma("sp", v_prompt.ap()[tok0:tok0 + 128, :], vst[sl][:], r=["vst%d" % sl], w=["vprompt%d_%d" % (s, i)])
            if STEP < 5:
                continue
            qk_to_bf16(sl)
            if STEP < 6:
                continue
            for cidx in range(6):
                P.op("pe", I.transpose(
                    out=psb[pb + 3][:, cidx * 128:(cidx + 1) * 128],
                    in_=qkb[sl][:, cidx * 128:(cidx + 1) * 128], identity=CBs("ident")),
                    r=["qkb%d" % sl, "cb"], w=["ps%d" % (pb + 3)])
            if STEP < 7:
                continue
            P.op("act", I.copy(out=qT[:, :, i * 128:(i + 1) * 128],
                                         in_=psb[pb + 3][:, 0:512].rearrange("p (c t) -> p c t", c=4)),
                 r=["ps%d" % (pb + 3)], w=["qT%d" % i])
            if STEP < 8:
                continue
            P.op("act", I.copy(out=kT[:, :, i * 128:(i + 1) * 128],
                                                in_=psb[pb + 3][:, 512:768].rearrange("p (c t) -> p c t", c=2)),
                 r=["ps%d" % (pb + 3)], w=["kT%d" % i])
        if stop_after == "p1a":
            break
        for oi, d in enumerate((1, 4, 16)):
            L = S // d
            for r_ in range(d):
                for n in range(L // 128):
                    tile = r_ * (L // 128) + n
                    row0 = s * S + r_ + d * 128 * n
                    src = dap(v_prompt, row0 * 256, [[d * 256, 128], [64, 4], [1, 64]])
                    lo_sub = (r_ + d * 128 * n) // 128
                    hi_sub = (r_ + d * 128 * n + d * 127) // 128
                    P.dma("pool", V3[:, oi, tile, :, 0:64], src, r=["vprompt%d_%d" % (s, ii) for ii in range(lo_sub, hi_sub + 1)],
                          w=["V3_%d_%d" % (oi, tile)])
        unit = 0
        for kv in range(4):
            c, hp = kv // 2, 64 * (kv % 2)
            first = True
            for oi, d in enumerate((1, 4, 16)):
                L = S // d
                nb = L // 128
                for r_ in range(d):
                    for n in range(nb):
                        u = unit % 3
                        unit += 1
                        bS, bO = u, 3 + u
                        t0 = r_ + d * 128 * n
                        tp = t0 - d * 128
                        q_ap = sap(qT, hp, 64, (2 * c) * S + t0, [[S, 2], [d, 128]])
                        kc_ap = sap(kT, hp, 64, c * S + t0, [[d, 128]])
                        ncol = 512 if n > 0 else 256
                        subs_c = sorted(set(range(t0 // 128, (t0 + d * 127) // 128 + 1)))
                        subs_p = sorted(set(range(tp // 128, (tp + d * 127) // 128 + 1))) if n > 0 else []
                        RQ = ["qT%d" % x for x in subs_c]
                        RKC = ["kT%d" % x for x in subs_c]
                        RKP = ["kT%d" % x for x in subs_p]
                        pe_mask = (unit % 2 == 0)
                        P.op("pe", I.matmul(
                            ps[bS][:, 0:256], lhsT=kc_ap, rhs=q_ap, start=True, stop=(not pe_mask)),
                            r=RKC + RQ, w=["ps%d" % bS])
                        if pe_mask:
                            P.op("pe", I.matmul(ps[bS][:, 0:256], lhsT=CBs("ident"), rhs=CBs("mcur"), start=False, stop=True),
                                 r=["cb"], w=["ps%d" % bS])
                        if n > 0:
                            kp_ap = sap(kT, hp, 64, c * S + tp, [[d, 128]])
                            P.op("pe", I.matmul(
                                ps[bS][:, 256:512], lhsT=kp_ap, rhs=q_ap, start=True, stop=(not pe_mask)),
                                r=RKP + RQ, w=["ps%d" % bS])
                            if pe_mask:
                                P.op("pe", I.matmul(ps[bS][:, 256:512], lhsT=CBs("ident"), rhs=CBs("mprev"), start=False, stop=True),
                                     r=["cb"], w=["ps%d" % bS])
                        P.op("act", I.activation(
                            out=ET[u][:, 0:ncol], in_=ps[bS][:, 0:ncol], func=AF.Exp),
                            r=["ps%d" % bS], w=["ET%d" % u])
                        if not pe_mask:
                            P.op("pool", I.tensor_tensor(out=ET[u][:, 0:ncol], in0=ET[u][:, 0:ncol], in1=CBs("m01")[:, 0:ncol], op=ALU.mult),
                                 r=["ET%d" % u, "cb"], w=["ET%d" % u])
                        tile = r_ * nb + n
                        P.op("pe", I.matmul(
                            ps[bO][0:65, 0:256], lhsT=V3[:, oi, tile, kv, :], rhs=ET[u][:, 0:256],
                            start=True, stop=(n == 0)),
                            r=["V3_%d_%d" % (oi, tile), "ET%d" % u], w=["ps%d" % bO])
                        if n > 0:
                            P.op("pe", I.matmul(
                                ps[bO][0:65, 0:256], lhsT=V3[:, oi, tile - 1, kv, :], rhs=ET[u][:, 256:512],
                                start=False, stop=True),
                                r=["V3_%d_%d" % (oi, tile - 1), "ET%d" % u], w=["ps%d" % bO])
                        a_ap = sap(acc, 0, 65, t0, [[S, 2], [d, 128]])
                        o_ap = ps[bO][0:65, 0:256].rearrange("p (g t) -> p g t", g=2)
                        if oi == 0:
                            P.op("dve", I.tensor_copy(out=a_ap, in_=o_ap),
                                 r=["ps%d" % bO], w=["acc"])
                        else:
                            P.op("dve", I.tensor_tensor(
                                out=a_ap, in0=a_ap, in1=o_ap, op=ALU.add),
                                r=["ps%d" % bO, "acc"], w=["acc"])
            P.op("dve", I.reciprocal(out=acc[64:65, :, :].rearrange("p g t -> p (g t)"),
                                               in_=acc[64:65, :, :].rearrange("p g t -> p (g t)")),
                 r=["acc"], w=["acc"])
            for g in range(2):
                for piece in range(4):
                    bB = 6 + (g * 4 + piece) % 2
                    col0 = g * S + piece * 512
                    P.op("pe", I.matmul(
                        ps[bB][0:64, 0:512], lhsT=CFs("ones", 64, 1)[:, 0:64], rhs=acc[64:65, g, piece * 512:(piece + 1) * 512],
                        start=True, stop=True), r=["cf", "acc"], w=["ps%d" % bB])
                    P.op("dve", I.tensor_tensor(
                        out=mixT[64 * g:64 * g + 64, kv, piece * 512:(piece + 1) * 512],
                        in0=acc[0:64, g, piece * 512:(piece + 1) * 512], in1=ps[bB][0:64, 0:512], op=ALU.mult),
                        r=["ps%d" % bB, "acc"], w=["mixA"])
        if debug_out:
            for kv in range(4):
                P.op("dve", I.tensor_copy(out=acc[:, 0, :], in_=mixT[:, kv, :]), r=["mixA", "acc"], w=["acc"])
                P.dma("sp", dbg.ap()[:, kv * S:(kv + 1) * S], acc[:, 0, :], r=["acc"], w=["dbg"])
        if stop_after == "att":
            break

        P.barrier()
        A2 = alloc_pass2()
        ssd_pass(A2, prompt=True, s=s)
        if stop_after == "p2":
            break
        P.barrier()


    def sample_attention():
        P.barrier()
        arena_reset()
        qk_ = [AT([128, 12, 64], F32)]
        rt_ = [AT([128, 4, 12, 8], F32)]
        vst_ = [AT([128, 256], F32)]
        qkb_ = [AT([128, 768], BF16)]
        qTs = AT([128, 4, 128], BF16)
        kTs = AT([128, 2, 128], BF16)
        Vn = AT([128, 4, 65], BF16)
        Kc = AT([128, 16, 256], BF16)
        Kf = AT([128, 16, 256], F32)
        Vf = AT([128, 16, 256], F32)
        KT = [AT([128, 2, 16, 128], BF16) for k in range(2)]
        Vx = [AT([128, 16, 4, 65], BF16) for k in range(2)]
        Es = [AT([128, 256], BF16) for k in range(2)]
        Ew = [AT([128, 256], BF16) for k in range(2)]
        accs = AT([128, 4, 256], F32)
        nonlocal qk, rt, vst, qkb
        sv = (qk, rt, vst, qkb)
        qk, rt, vst, qkb = qk_, rt_, vst_, qkb_
        rmsnorm_to_xT(x_sample.ap()[0:128, :], 0, gmix, "gmix")
        for half, bank in ((0, 1), (1, 2)):
            for k in range(8):
                P.op("pe", I.matmul(ps[bank][:, 0:512], lhsT=xT[0][:, k, :], rhs=win[:, k, half * 512:(half + 1) * 512],
                                    start=(k == 0), stop=(k == 7)), r=["xT0", WINA[k]], w=["ps%d" % bank])
        oc, _w = CF["cosS"]
        os_, _w = CF["sinS"]
        rope_qk(0, cf[:, oc:oc + 8], cf[:, os_:os_ + 8], 1, 2)
        P.op("act", I.copy(out=vst[0][:], in_=ps[2][:, 256:512]), r=["ps2"], w=["vst0"])
        for b in range(NSB):
            P.dma("sp", k_sample.ap()[b:b + 1, 2040 * 256:S * 256].rearrange("o (t f) -> (o t) f", f=256),
                  qk[0][b * 8:(b + 1) * 8, 8:12, :].rearrange("p h d -> p (h d)"), r=["qk0"])
            P.dma("sp", v_sample.ap()[b:b + 1, 2040 * 256:S * 256].rearrange("o (t f) -> (o t) f", f=256),
                  vst[0][b * 8:(b + 1) * 8, :], r=["vst0"])
        qk_to_bf16(0)
        for cidx in range(6):
            P.op("pe", I.transpose(out=psb[3][:, cidx * 128:(cidx + 1) * 128], in_=qkb[0][:, cidx * 128:(cidx + 1) * 128],
                                   identity=CBs("ident")), r=["qkb0", "cb"], w=["ps3"])
        P.op("act", I.copy(out=qTs[:], in_=psb[3][:, 0:512].rearrange("p (c t) -> p c t", c=4)), r=["ps3"], w=["qTs"])
        P.op("act", I.copy(out=kTs[:], in_=psb[3][:, 512:768].rearrange("p (c t) -> p c t", c=2)), r=["ps3"], w=["kTs"])
        P.op("pool", I.memset(Vn[:].rearrange("p a b -> p (a b)"), 1.0), w=["Vn"])
        P.op("pool", I.tensor_copy(out=Vn[:, :, 0:64], in_=vst[0][:].rearrange("p (a b) -> p a b", a=4)), r=["vst0", "Vn"], w=["Vn"])
        for u in range(2):
            P.op("pool", I.memset(Vx[u][:].rearrange("p a b c -> p (a b c)"), 1.0), w=["Vx%d" % u])
        for kv in range(4):
            c, hp = kv // 2, 64 * (kv % 2)
            u = kv % 2
            q_ap = sap(qTs, hp, 64, 2 * c * 128, [[8, 16], [128, 2], [1, 8]])
            P.op("pe", I.matmul(ps[2 + u][:, 0:256], lhsT=kTs[hp:hp + 64, c, :], rhs=q_ap, start=True, stop=True),
                 r=["kTs", "qTs"], w=["ps%d" % (2 + u)])
            P.op("act", I.activation(out=Es[u][:], in_=ps[2 + u][:, 0:256], func=AF.Exp), r=["ps%d" % (2 + u)], w=["Es%d" % u])
            P.op("dve", I.tensor_tensor(out=Ew[u][:], in0=Es[u][:], in1=CBs("wnew"), op=ALU.mult), r=["Es%d" % u, "cb"], w=["Ew%d" % u])
            bn = 6 + kv // 2
            cn = (kv % 2) * 256
            P.op("pe", I.matmul(ps[bn][0:65, cn:cn + 256], lhsT=Vn[:, kv, :], rhs=Ew[u][:], start=True, stop=True),
                 r=["Vn", "Ew%d" % u], w=["ps%d" % bn])
        P.op("pool", I.memset(Kf[:].rearrange("p a b -> p (a b)"), 0.0), w=["Kf"])
        P.op("pool", I.memset(Vf[:].rearrange("p a b -> p (a b)"), 0.0), w=["Vf"])
        for b in range(NSB):
            ub = b % 2
            for (ct, Xf, tg) in ((cache_k, Kf, "Kf"), (cache_v, Vf, "Vf")):
                P.dma("sp", Xf[:, 0:8, :].rearrange("p a b -> p (a b)"), dap(ct, b * S * 256, [[4096, 128], [1, 2048]]), w=[tg])
                P.dma("sp", Xf[96:128, 8:16, :].rearrange("p a b -> p (a b)"),
                      dap(ct, b * S * 256 + 96 * 4096 + 2048, [[4096, 32], [1, 2048]]), w=[tg])
            P.op("act", I.copy(out=Kc[:].rearrange("p a b -> p (a b)"), in_=Kf[:].rearrange("p a b -> p (a b)")), r=["Kf"], w=["Kc"])
            P.op("dve", I.tensor_copy(out=Vx[ub][:, :, :, 0:64], in_=Vf[:].rearrange("p r (k d) -> p r k d", k=4)),
                 r=["Vf", "Vx%d" % ub], w=["Vx%d" % ub])
            for c in range(2):
                for rh in range(2):
                    bank = 2 + (c * 2 + rh) % 2
                    for rr in range(8):
                        r_ = rh * 8 + rr
                        P.op("pe", I.transpose(out=psb[bank][:, rr * 128:(rr + 1) * 128], in_=Kc[:, r_, c * 128:(c + 1) * 128],
                                               identity=CBs("ident")), r=["Kc", "cb"], w=["ps%d" % bank])
                    P.op("act", I.copy(out=KT[ub][:, c, rh * 8:(rh + 1) * 8, :],
                                       in_=psb[bank][:, 0:1024].rearrange("p (r m) -> p r m", r=8)),
                         r=["ps%d" % bank], w=["KT%d" % ub])
            for kv in range(4):
                c, hp = kv // 2, 64 * (kv % 2)
                u = kv % 2
                q_ap = sap(qTs, hp, 64, 2 * c * 128 + b * 8, [[128, 2], [1, 8]])
                for r_ in range(16):
                    P.op("pe", I.matmul(ps[u][:, r_ * 16:(r_ + 1) * 16], lhsT=KT[ub][hp:hp + 64, c, r_, :], rhs=q_ap,
                                        start=True, stop=True), r=["KT%d" % ub, "qTs"], w=["ps%d" % u])
                P.op("act", I.activation(out=Es[u][:], in_=ps[u][:, 0:256], func=AF.Exp), r=["ps%d" % u], w=["Es%d" % u])
                P.op("dve", I.tensor_tensor(out=Ew[u][:], in0=Es[u][:], in1=CBs("wcache"), op=ALU.mult),
                     r=["Es%d" % u, "cb"], w=["Ew%d" % u])
                bn = 4 + kv // 2
                cn = (kv % 2) * 256 + b * 16
                for r_ in range(16):
                    P.op("pe", I.matmul(ps[bn][0:65, cn:cn + 16], lhsT=Vx[ub][:, r_, kv, :], rhs=Ew[u][:, r_ * 16:(r_ + 1) * 16],
                                        start=(r_ == 0), stop=(r_ == 15)), r=["Vx%d" % ub, "Ew%d" % u], w=["ps%d" % bn])
        for kv in range(4):
            bn, cn = 6 + kv // 2, (kv % 2) * 256
            bc_, cc = 4 + kv // 2, (kv % 2) * 256
            P.op("act", I.copy(out=accs[0:65, kv, :], in_=ps[bn][0:65, cn:cn + 256]), r=["ps%d" % bn], w=["accs%d" % kv])
            P.op("dve", I.tensor_tensor(out=accs[0:65, kv, :], in0=accs[0:65, kv, :], in1=ps[bc_][0:65, cc:cc + 256], op=ALU.add),
                 r=["accs%d" % kv, "ps%d" % bc_], w=["accs%d" % kv])
            P.op("dve", I.reciprocal(out=accs[64:65, kv, :], in_=accs[64:65, kv, :]), r=["accs%d" % kv], w=["accs%d" % kv])
            P.op("pe", I.matmul(ps[kv % 2][0:64, 0:256], lhsT=CFs("ones", 64, 1)[:, 0:64], rhs=accs[64:65, kv, :], start=True, stop=True),
                 r=["cf", "accs%d" % kv], w=["ps%d" % (kv % 2)])
            for g in range(2):
                in0 = sap(accs, 0, 64, kv * 256 + g * 8, [[16, 16], [1, 8]])
                in1 = sap(ps[kv % 2], 0, 64, g * 8, [[16, 16], [1, 8]])
                P.op("dve", I.tensor_tensor(out=mixT[64 * g:64 * g + 64, kv, 0:128].rearrange("p (b t) -> p b t", b=16),
                                            in0=in0, in1=in1, op=ALU.mult),
                     r=["accs%d" % kv, "ps%d" % (kv % 2)], w=["mixA"])
        qk, rt, vst, qkb = sv

    def sample_ssd():
        P.barrier()
        A = alloc_pass2(sample=True)
        A["xT5"] = xT[0]
        pre_s = AT([128, 8, 16, 11], F32)
        stc = AT([128, D], F32)
        h0n = [AT([128, 4, 128], F32) for k in range(2)]
        h0T = [AT([128, 512], BF16) for k in range(2)]
        CTm = AT([128, 2, 16, 128], BF16)
        xwm = [AT([128, 512], BF16) for k in range(2)]
        decq = AT([128, 4, 16], F32)
        abc2 = AT([128, 8, 64], F32)
        stout = [AT([128, 4, 128], F32) for k in range(2)]
        xbcT, cacc, xtok, tm = A["xbcT"][0], A["cacc"], A["xtok"], A["tm"]
        sm, aa, xw = A["sm"][0], A["aa"][0], A["xw"][0]
        P.dma("sp", stc[0:48, :], state_conv.ap()[0:48, :], w=["stc"])
        for ch in range(8):
            P.op("pe", I.transpose(out=ps[4][:, ch * 48:(ch + 1) * 48], in_=stc[0:48, ch * 128:(ch + 1) * 128],
                                   identity=CFs("ident", 0, 48)[:, 0:48]), r=["stc", "cf"], w=["ps4"])
        P.op("act", I.copy(out=pre_s[:, :, :, 0:3], in_=ps[4][:, 0:384].rearrange("p (c b r) -> p c b r", c=8, b=16)),
             r=["ps4"], w=["pre_s"])
        for ch in range(8):
            bank = ch // 4
            for k in range(8):
                P.op("pe", I.matmul(ps[bank][:, (ch % 4) * 128:(ch % 4 + 1) * 128], lhsT=win[:, k, 1536 + ch * 128:1536 + (ch + 1) * 128],
                                    rhs=xT[0][:, k, :], start=(k == 0), stop=(k == 7)), r=["xT0", WIN[k]], w=["ps%d" % bank])
        for bank in range(2):
            P.op("act", I.copy(out=pre_s[:, bank * 4:(bank + 1) * 4, :, 3:11],
                               in_=ps[bank][:, 0:512].rearrange("p (c b t) -> p c b t", c=4, b=16)),
                 r=["ps%d" % bank, "pre_s"], w=["pre_s"])
        for ch in range(8):
            u = ch % 2
            cv = cacc[u][:, 0:128].rearrange("p (b t) -> p b t", b=16)
            P.op("dve", I.tensor_scalar(out=cv, in0=pre_s[:, ch, :, 3:11], scalar1=cw[:, 3, ch:ch + 1], scalar2=cbias[:, ch:ch + 1],
                                        op0=ALU.mult, op1=ALU.add), r=["pre_s", "cw", "cbias"], w=["cacc%d" % u])
            for tap in (2, 1, 0):
                P.op("dve", I.scalar_tensor_tensor(out=cv, in0=pre_s[:, ch, :, tap:tap + 8], scalar=cw[:, tap, ch:ch + 1],
                                                   in1=cv, op0=ALU.mult, op1=ALU.add),
                     r=["pre_s", "cw", "cacc%d" % u], w=["cacc%d" % u])
            P.op("act", I.activation(out=xbcT[:, ch, 0:128], in_=cacc[u][:, 0:128], func=AF.Silu), r=["cacc%d" % u],
                 w=["xbcT0_%d_0" % ch])
        for half in range(2):
            for k in range(8):
                P.op("pe", I.matmul(ps[4 + half][:, 0:512], lhsT=xT[0][:, k, :], rhs=win[:, k, 1536 + half * 512:1536 + (half + 1) * 512],
                                    start=(k == 0), stop=(k == 7)), r=["xT0", WIN[k]], w=["ps%d" % (4 + half)])
            P.op("act", I.copy(out=xtok[:, half * 512:(half + 1) * 512], in_=ps[4 + half][:, 0:512]),
                 r=["ps%d" % (4 + half)], w=["decT_0"])
        for r_ in range(3):
            P.dma("sp", dap(conv_sample, r_ * 1024, [[3 * 1024, 16], [1, 1024]]), sap(xtok, 5 + r_, 1, 0, [[1, 1024]]) if False else
                  bass.AP(xtok.tensor, xtok.offset + (5 + r_) * xtok.ap[0][0], [[8 * xtok.ap[0][0], 16], [1, 1024]]), r=["decT_0"])

        def yoff_emit():
            P.op("dve", I.tensor_copy(out=abc2[:], in_=bc8(aa, 64)), r=["aa_0"], w=["abc2"])
            for q in range(4):
                P.op("pe", I.matmul(ps[3][:, 32 + q * 16:32 + (q + 1) * 16], lhsT=abc2[:, 2 * q:2 * q + 2, :].rearrange("p a b -> p (a b)"),
                                    rhs=CFs("blkcol"), start=True, stop=True), r=["abc2", "cf"], w=["ps3"])
            P.op("act", I.activation(out=decq[:].rearrange("p a b -> p (a b)"), in_=ps[3][:, 32:96], func=AF.Exp), r=["ps3"], w=["decq"])
            for g in range(2):
                xin = bass.AP(xbcT.tensor, xbcT.offset + (6 + g) * 512, [list(xbcT.ap[0]), [0, 16], [1, 128]])
                P.op("dve", I.tensor_tensor(out=CTm[:, g], in0=xin, in1=CBs("blkrow").rearrange("p (b t) -> p b t", b=16), op=ALU.mult),
                     r=["xbcT0_%d_0" % (6 + g), "cb"], w=["CTm"])
            for b in range(NSB):
                ub = b % 2
                P.dma("sp", h0n[ub][:], dap(state_ssm, b * 512 * 128, [[128, 128], [128 * 128, 4], [1, 128]]), w=["h0n%d" % ub])
                bt = 4 + ub
                for q in range(4):
                    P.op("pe", I.transpose(out=ps[bt][:, q * 128:(q + 1) * 128], in_=h0n[ub][:, q, :], identity=CFs("ident")),
                         r=["h0n%d" % ub, "cf"], w=["ps%d" % bt])
                P.op("act", I.copy(out=h0T[ub][:], in_=ps[bt][:, 0:512]), r=["ps%d" % bt], w=["h0T%d" % ub])
                for g in range(2):
                    yo_ap, yo_tag = ((ps[0][:, 0:256], "ps0"), (ps[6][:, 256:512], "ps6"))[g]
                    P.op("pe", I.matmul(yo_ap, lhsT=CTm[:, g, b, :], rhs=h0T[ub][:, g * 256:(g + 1) * 256],
                                        start=(b == 0), stop=(b == NSB - 1)), r=["CTm", "h0T%d" % ub], w=[yo_tag])
                P.op("dve", I.tensor_scalar(out=xwm[ub][:], in0=xw[:], scalar1=CFs("blkcol")[:, b:b + 1], scalar2=None, op0=ALU.mult),
                     r=["xw_0", "cf"], w=["xwm%d" % ub])
                bs = 1 + ub
                for q in range(4):
                    g = q // 2
                    P.op("pe", I.matmul(ps[bs][:, q * 128:(q + 1) * 128], lhsT=xwm[ub][:, q * 128:(q + 1) * 128],
                                        rhs=tm[:, 0, 512 + g * 128:512 + (g + 1) * 128], start=True, stop=True),
                         r=["xwm%d" % ub, "tm0"], w=["ps%d" % bs])
                for q in range(4):
                    P.op("dve", I.scalar_tensor_tensor(out=stout[ub][:, q, :], in0=h0n[ub][:, q, :], scalar=decq[:, q, b:b + 1],
                                                       in1=ps[bs][:, q * 128:(q + 1) * 128], op0=ALU.mult, op1=ALU.add),
                         r=["h0n%d" % ub, "decq", "ps%d" % bs], w=["stout%d" % ub])
                P.dma("sp", dap(ssm_sample, b * 512 * 128, [[128, 128], [128 * 128, 4], [1, 128]]), stout[ub][:], r=["stout%d" % ub])

        ssd_chunk(A, 0, 0, None, True, "triB", "onesB", "dmaskBF", x_sample.ap()[0:128, :], NTOK, False, yoff_emit=yoff_emit)

    cache_copy_some(len(copy_jobs))
    if stop_after is None or stop_after == "sample":
        sample_attention()
        sample_ssd()

    def phase_b(tiles):
        P.barrier()
        arena_pos[0] = 0
        wup = AT([128, 8, 4096], BF16)
        wdn = AT([128, 32, D], BF16)
        gfin = AT([128, D], F32)
        hnT = AT([128, 8, 256], BF16)
        u2T = AT([128, 32, 256], BF16)
        rtmp = [AT([128, 512], F32) for k in range(2)]
        yout = [AT([128, D], F32) for k in range(2)]
        hx = [xs[0], xs[1], xres[0], AT([128, D], F32)]
        ssqB = [T("ssqB%d" % k, [128, 4], F32) for k in range(4)]
        ssqF = [T("ssqF%d" % k, [128, 4], F32) for k in range(2)]
        for k in range(8):
            P.dma("pool", wup[:, k, :], w_up.ap()[k * 128:(k + 1) * 128, :], w=["wup%d" % k])
        for q in range(8):
            P.dma("pool", wdn[:, q * 4:(q + 1) * 4, :], dap(w_down, q * 4 * 128 * D, [[D, 128], [128 * D, 4], [1, D]]),
                  w=["wdn%d" % q])
        P.dma("sp", gmix[:], dap(norm_mlp, 0, [[0, 128], [1, D]]), w=["gmix"])
        P.dma("sp", gfin[:], dap(norm_final, 0, [[0, 128], [1, D]]), w=["gfin"])
        for ti, (row0, nsub, dst_t, dst_row0) in enumerate(tiles):
            ntok = nsub * 128
            par = ti % 2
            for sub in range(nsub):
                hs = 2 * par + sub
                r0 = row0 + sub * 128
                P.dma("sp", hx[hs][:], hbuf.ap()[r0:r0 + 128, :], r=["hbuf_%d" % r0], w=["hx%d" % hs])
                sl = sub
                sqb, sqn = ssqB[hs], "ssqB%d" % hs
                P.op("pool", I.memset(sqb[:], 0.0), w=[sqn])
                P.op("act", I.activation(out=xn[sl][:], in_=hx[hs][:], func=AF.Square, accum_out=sqb[:, 0:1]),
                     r=["hx%d" % hs, sqn], w=["xn%d" % sl, sqn])
                rstd_ops(sqb, sqn, 1.0 / D)
                P.op("dve", I.scalar_tensor_tensor(out=xn[sl][:], in0=hx[hs][:], scalar=sqb[:, 2:3], in1=gmix[:],
                                                   op0=ALU.mult, op1=ALU.mult), r=["hx%d" % hs, sqn, "gmix"], w=["xn%d" % sl])
                pb = 2 + sub
                for k in range(8):
                    P.op("pe", I.transpose(out=psb[pb][:, k * 128:(k + 1) * 128], in_=xn[sl][:, k * 128:(k + 1) * 128],
                                           identity=CBs("ident")), r=["xn%d" % sl, "cb"], w=["ps%d" % pb])
                P.op("act", I.copy(out=hnT[:, :, sub * 128:(sub + 1) * 128], in_=psb[pb][:, 0:1024].rearrange("p (k t) -> p k t", k=8)),
                     r=["ps%d" % pb], w=["hnT%d" % sub])
            HN = ["hnT%d" % x for x in range(nsub)]
            for cp in range(16):
                bank = cp % 4
                u = cp % 2
                for j in range(2):
                    c = 2 * cp + j
                    for k in range(8):
                        P.op("pe", I.matmul(ps[bank][:, j * 256:j * 256 + ntok], lhsT=wup[:, k, c * 128:(c + 1) * 128],
                                            rhs=hnT[:, k, 0:ntok], start=(k == 0), stop=(k == 7)),
                             r=HN + ["wup%d" % k], w=["ps%d" % bank])
                pv = ps[bank][:, 0:512].rearrange("p (j t) -> p j t", j=2)[:, :, 0:ntok]
                rv = rtmp[u][:].rearrange("p (j t) -> p j t", j=2)[:, :, 0:ntok]
                P.op("act", I.activation(out=rv, in_=pv, func=AF.Relu), r=["ps%d" % bank], w=["rtmp%d" % u])
                eng = "dve" if cp % 2 == 0 else "pool"
                P.op(eng, I.tensor_tensor(out=u2T[:, 2 * cp:2 * cp + 2, 0:ntok], in0=rv, in1=rv, op=ALU.mult),
                     r=["rtmp%d" % u], w=["u2T%d" % cp])
            U2 = ["u2T%d" % x for x in range(16)]
            for sub in range(nsub):
                hs = 2 * par + sub
                yo = yout[sub]
                for half in range(2):
                    bank = 4 + 2 * sub + half
                    for c in range(32):
                        P.op("pe", I.matmul(ps[bank][:, 0:512], lhsT=u2T[:, c, sub * 128:(sub + 1) * 128],
                                            rhs=wdn[:, c, half * 512:(half + 1) * 512], start=(c == 0), stop=(c == 31)),
                             r=["u2T%d" % (c // 2), "wdn%d" % (c // 4)], w=["ps%d" % bank])
                    P.op("dve", I.tensor_tensor(out=yo[:, half * 512:(half + 1) * 512], in0=ps[bank][:, 0:512],
                                                in1=hx[hs][:, half * 512:(half + 1) * 512], op=ALU.add),
                         r=["ps%d" % bank, "hx%d" % hs], w=["yout%d" % sub])
                sq, sqn = ssqF[sub], "ssqF%d" % sub
                P.op("pool", I.memset(sq[:], 0.0), w=[sqn])
                P.op("act", I.activation(out=hx[hs][:], in_=yo[:], func=AF.Square, accum_out=sq[:, 0:1]),
                     r=["yout%d" % sub, sqn], w=["hx%d" % hs, sqn])
                rstd_ops(sq, sqn, 1.0 / D)
                P.op("dve", I.scalar_tensor_tensor(out=yo[:], in0=yo[:], scalar=sq[:, 2:3], in1=gfin[:], op0=ALU.mult, op1=ALU.mult),
                     r=["yout%d" % sub, sqn, "gfin"], w=["yout%d" % sub])
                dr = dst_row0 + sub * 128
                P.dma("sp", dst_t.ap()[dr:dr + 128, :], yo[:], r=["yout%d" % sub])

    if stop_after in ("p2",):
        phase_b([(t * 256, 2, y_prompt, t * 256) for t in range(8)])
    elif stop_after == "sample":
        phase_b([(NTOK, 1, y_sample, 0)])
    elif stop_after is None:
        tiles = [(t * 256, 2, y_prompt, t * 256) for t in range(NTOK // 256)]
        tiles.append((NTOK, 1, y_sample, 0))
        phase_b(tiles)

    counts = P.emit()
    st.close()
    return nc, counts


def _core_inputs(inputs, c, cf, cbv):
    f = lambda a: np.ascontiguousarray(a, dtype=np.float32)
    m = {
        "x_prompt": f(inputs["x_prompt"][2 * c:2 * c + 2]).reshape(2 * S, D),
        "x_sample": f(inputs["x_sample"][16 * c:16 * c + 16]).reshape(128, D),
        "cache_k": f(inputs["cache_k"][0, 16 * c:16 * c + 16]).reshape(16, S * 256),
        "cache_v": f(inputs["cache_v"][0, 16 * c:16 * c + 16]).reshape(16, S * 256),
        "state_conv": f(inputs["state_conv"][0, 16 * c:16 * c + 16]).reshape(48, 1024),
        "state_ssm": f(inputs["state_ssm"][0, 16 * c:16 * c + 16]).reshape(16 * 512, 128),
        "w_in": f(inputs["w_in"][0]), "w_out": f(inputs["w_out"][0]),
        "conv_w": f(inputs["conv_w"][0]), "conv_b": f(inputs["conv_b"][0]).reshape(1, 1024),
        "dt_bias": f(inputs["dt_bias"][0]).reshape(1, 8), "a_log": f(inputs["a_log"][0]).reshape(1, 8),
        "d_skip": f(inputs["d_skip"][0]).reshape(1, 8), "ssm_norm": f(inputs["ssm_norm"][0]).reshape(1, 512),
        "norm_mix": f(inputs["norm_mix"][0]).reshape(1, D), "norm_mlp": f(inputs["norm_mlp"][0]).reshape(1, D),
        "w_up": f(inputs["w_up"][0]), "w_down": f(inputs["w_down"][0]),
        "norm_final": f(inputs["norm_final"]).reshape(1, D),
        "cf": cf, "cb": cbv,
    }
    return m


def kernel(**inputs):
    cf, cbv = make_consts()
    nc, _ = build()
    in_maps = [_core_inputs(inputs, c, cf, cbv) for c in range(NCORES)]
    res = run_bass_kernel_spmd(nc, in_maps, core_ids=list(range(NCORES)))
    R = res.results
    cat = lambda name: np.concatenate([np.asarray(r[name]) for r in R], axis=0)
    y_prompt = cat("y_prompt").reshape(16, S, D)
    y_sample = cat("y_sample").reshape(128, 8, D)
    k_prompt = cat("k_prompt").reshape(1, 16, S, 4, 64)
    v_prompt = cat("v_prompt").reshape(1, 16, S, 4, 64)
    conv_prompt = cat("conv_prompt").reshape(1, 16, 3, 1024)
    ssm_prompt = cat("ssm_prompt").reshape(1, 16, 8, 64, 128)
    k_sample = cat("k_sample").reshape(1, 128, S, 4, 64)
    v_sample = cat("v_sample").reshape(1, 128, S, 4, 64)
    conv_sample = cat("conv_sample").reshape(1, 128, 3, 1024)
    ssm_sample = cat("ssm_sample").reshape(1, 128, 8, 64, 128)
    return (y_prompt, y_sample, k_prompt, v_prompt, conv_prompt, ssm_prompt,
            k_sample, v_sample, conv_sample, ssm_sample)
```

```python
import contextlib
import os
import numpy as np
import ml_dtypes
import concourse.bass as bass
import concourse.mybir as mybir
from concourse.bass_utils import run_bass_kernel_spmd

F32 = mybir.dt.float32
BF16 = mybir.dt.bfloat16
AF = mybir.ActivationFunctionType
ALU = mybir.AluOpType

ENGS = ("pe", "act", "dve", "pool", "sp")
NCORES = 8
S = 2048
D = 1024
NEG = -30000.0
EPS = 1e-5


class _Op:
    __slots__ = ("eng", "fn", "deps", "seq", "gidx", "is_dma", "sem", "val", "prev_val",
                 "needs_signal", "sig_val", "waits", "cost", "seg", "start", "finish", "nsucc", "succ", "tset")


def _prod(shape):
    n = 1
    for v in shape:
        n *= int(v)
    return n


class _Rec:
    def __getattr__(self, name):
        def mk(*a, **kw):
            fn = lambda e: getattr(e, name)(*a, **kw)
            fn.meth = name
            fn.a = a
            fn.kw = kw
            return fn
        return mk


I = _Rec()


def _est_cost(eng, fn):
    name = getattr(fn, "meth", "")
    a, kw = getattr(fn, "a", ()), getattr(fn, "kw", {})
    out = kw.get("out", a[0] if a else None)
    try:
        elems = _prod(out.shape[1:])
    except Exception:
        elems = 512
    if eng == "pe":
        if name == "transpose":
            return 135.0
        lhsT = kw.get("lhsT")
        f = 3.0 if (lhsT is not None and lhsT.dtype == F32) else 1.0
        return (45.0 + 0.47 * elems) * f
    if eng == "act":
        return 200.0 + 0.45 * elems
    if eng == "dve":
        return 70.0 + 1.05 * elems
    if eng == "pool":
        return 120.0 + 2.1 * elems
    return 100.0


class Prog:
    def __init__(self, nc, n_dma_sems=20, reorder=True):
        self.nc = nc
        self.all = []
        self.last_w = {}
        self.readers = {}
        self.n_dma_sems = n_dma_sems
        self.seg = 0
        self.reorder = reorder

    def barrier(self):
        self.seg += 1

    def _record(self, o, r, w, after=()):
        deps = set(after)
        for x in r:
            lw = self.last_w.get(x)
            if lw is not None:
                deps.add(lw)
        for x in w:
            lw = self.last_w.get(x)
            if lw is not None:
                deps.add(lw)
            for rd in self.readers.get(x, ()):
                deps.add(rd)
        deps.discard(o)
        o.deps = deps
        for x in w:
            self.last_w[x] = o
            self.readers[x] = []
        for x in r:
            self.readers.setdefault(x, []).append(o)
        o.gidx = len(self.all)
        o.seg = self.seg
        o.needs_signal = False
        o.waits = []
        o.sig_val = 0
        o.sem = None; o.val = 0; o.prev_val = 0
        self.all.append(o)
        return o

    def op(self, eng, fn, r=(), w=(), c=None):
        o = _Op()
        o.eng = eng; o.fn = fn; o.is_dma = False
        o.tset = None
        if eng == "act" and getattr(fn, "meth", "") == "activation":
            f_ = fn.kw.get("func")
            if f_ in (AF.Exp, AF.Ln):
                o.tset = "exp"
            elif f_ == AF.Silu:
                o.tset = "silu"
            elif f_ == AF.Sqrt:
                o.tset = "sqrt"
        o.cost = c if c is not None else _est_cost(eng, fn)
        return self._record(o, r, w)

    def dma(self, queue, out, in_, r=(), w=(), after=(), **kw):
        o = _Op()
        o.eng = queue; o.is_dma = True
        o.tset = None
        try:
            nbytes = _prod(out.shape) * (4 if out.dtype == F32 else 2)
        except Exception:
            nbytes = 65536
        o.cost = 2500.0 + nbytes / 120.0
        o.fn = I.dma_start(out=out, in_=in_, **kw)
        return self._record(o, r, w, after)

    def _schedule_segment(self, ops, t_base):
        import heapq
        inseg = set(ops)
        for o in ops:
            o.succ = []
        for o in ops:
            n = 0
            for d in o.deps:
                if d in inseg:
                    d.succ.append(o)
                    n += 1
            o.nsucc = n
            o.start = t_base
        eng_time = {e: t_base for e in ENGS}
        fut = {e: [] for e in ENGS}
        now = {e: [] for e in ENGS}
        for o in ops:
            if o.nsucc == 0:
                heapq.heappush(fut[o.eng], (o.start, o.gidx, o))
        order = []
        act_set = [None]
        remaining = len(ops)
        while remaining:
            best = None
            for e in ENGS:
                f, n_ = fut[e], now[e]
                while f and f[0][0] <= eng_time[e]:
                    rt, gi, oo = heapq.heappop(f)
                    heapq.heappush(n_, (gi, oo))
                if n_:
                    cand = (eng_time[e], n_[0][0], e, True)
                elif f:
                    cand = (f[0][0], f[0][1], e, False)
                else:
                    continue
                if best is None or cand < best:
                    best = cand
            st_, gi, e, from_now = best
            if from_now:
                if e == "act" and len(now[e]) > 1:
                    cands = heapq.nsmallest(6, now[e])
                    pick = None
                    for cnd in cands:
                        if cnd[1].tset is None or cnd[1].tset == act_set[0]:
                            pick = cnd
                            break
                    if pick is None:
                        pick = cands[0]
                    now[e].remove(pick)
                    heapq.heapify(now[e])
                    o = pick[1]
                else:
                    _, o = heapq.heappop(now[e])
            else:
                _, _, o = heapq.heappop(fut[e])
            if e == "act" and o.tset is not None:
                if act_set[0] != o.tset:
                    st_ += 1300.0
                act_set[0] = o.tset
            o.start = st_
            if o.is_dma:
                issue = 900.0 if e == "pool" else 70.0
                o.finish = st_ + o.cost
                eng_time[e] = st_ + issue
            else:
                o.finish = st_ + o.cost
                eng_time[e] = o.finish
            order.append(o)
            remaining -= 1
            for sc in o.succ:
                lat = 0.0 if (sc.eng == e and e == "pe") else 120.0
                if o.finish + lat > sc.start:
                    sc.start = o.finish + lat
                sc.nsucc -= 1
                if sc.nsucc == 0:
                    heapq.heappush(fut[sc.eng], (sc.start, sc.gidx, sc))
        t_end = max([eng_time[e] for e in ENGS] + [o.finish for o in ops])
        return order, t_end

    def emit(self):
        nc = self.nc
        segs = {}
        for o in self.all:
            segs.setdefault(o.seg, []).append(o)
        glob = []
        t = 0.0
        for k in sorted(segs):
            if self.reorder:
                t_prev = t
                order, t = self._schedule_segment(segs[k], t)
                busy = {e: sum(o.cost for o in order if o.eng == e and not o.is_dma) for e in ENGS}
                print("[sched] seg %d: %d ops (pe %d), est %.0f us; busy(us): %s" % (
                    k, len(order), sum(1 for o in order if o.eng == "pe"), (t - t_prev) / 1e3, {e: round(v / 1e3) for e, v in busy.items()}))
            else:
                order = segs[k]
            glob.append(order)
        pos = {}
        for order in glob:
            for o in order:
                pos[o] = len(pos)
        for order in glob:
            for o in order:
                for d in o.deps:
                    assert pos[d] < pos[o], ("schedule violates dependency", d.gidx, o.gidx)
        self.ops = {e: [] for e in ENGS}
        for order in glob:
            for o in order:
                o.seq = len(self.ops[o.eng])
                self.ops[o.eng].append(o)
        dma_sem_vals = {}
        rr = {e: 0 for e in ENGS}
        for e in ENGS:
            for o in self.ops[e]:
                if o.is_dma:
                    slot = rr[e] % self.n_dma_sems
                    rr[e] += 1
                    key = (e, slot)
                    prev = dma_sem_vals.get(key, 0)
                    o.sem = key; o.prev_val = prev; o.val = prev + 16
                    dma_sem_vals[key] = o.val
        waited_seq = {f: {e: -1 for e in ENGS} for f in ENGS}
        waited_dma = {f: {} for f in ENGS}
        fence_eng = None
        fence_dma = {}
        pending_fence = set()
        run_dma = {}
        last_op = {}
        for si, order in enumerate(glob):
            if si > 0:
                fence_eng = dict(last_op)
                fence_dma = dict(run_dma)
                pending_fence = set(ENGS)
            for o in order:
                f = o.eng
                rest = []
                best = {}
                if f in pending_fence:
                    pending_fence.discard(f)
                    for e, d in fence_eng.items():
                        if e == f:
                            continue
                        if waited_seq[f][e] < d.seq:
                            best[e] = d
                    for key, v in fence_dma.items():
                        if waited_dma[f].get(key, 0) < v:
                            rest.append(("dma", key, v))
                            waited_dma[f][key] = v
                if o.is_dma and o.prev_val > 0:
                    if waited_dma[f].get(o.sem, 0) < o.prev_val:
                        rest.append(("dma", o.sem, o.prev_val))
                        waited_dma[f][o.sem] = o.prev_val
                for d in sorted(o.deps, key=lambda x: x.gidx):
                    if d.is_dma:
                        if waited_dma[f].get(d.sem, 0) < d.val:
                            rest.append(("dma", d.sem, d.val))
                            waited_dma[f][d.sem] = d.val
                    else:
                        e = d.eng
                        if e == "pe" and f == "pe":
                            continue
                        if waited_seq[f][e] >= d.seq:
                            continue
                        if e not in best or best[e].seq < d.seq:
                            best[e] = d
                for e, d in best.items():
                    d.needs_signal = True
                    waited_seq[f][e] = d.seq
                    rest.append(("eng", d))
                o.waits = rest
                if o.is_dma:
                    run_dma[o.sem] = o.val
                else:
                    last_op[f] = o
        for e in ENGS:
            c = 0
            for o in self.ops[e]:
                if (not o.is_dma) and o.needs_signal:
                    c += 1
                    o.sig_val = c
        final_waits = list(dma_sem_vals.items())
        with contextlib.ExitStack() as st:
            esem = {e: st.enter_context(nc.semaphore("es_" + e)) for e in ENGS}
            dsem = {k: st.enter_context(nc.semaphore("ds_%s_%d" % k)) for k in dma_sem_vals}
            block = st.enter_context(nc.Block())

            def run(name, e):
                for o in self.ops[name]:
                    for wt in o.waits:
                        if wt[0] == "dma":
                            e.wait_ge(dsem[wt[1]], wt[2])
                        else:
                            e.wait_ge(esem[wt[1].eng], wt[1].sig_val)
                    ins = o.fn(e)
                    if o.is_dma:
                        ins.then_inc(dsem[o.sem], 16)
                    elif o.needs_signal:
                        ins.then_inc(esem[name], 1)
                if name == "sp":
                    for k, v in final_waits:
                        e.wait_ge(dsem[k], v)

            @block.sync
            def _(e):
                run("sp", e)

            @block.tensor
            def _(e):
                run("pe", e)

            @block.scalar
            def _(e):
                run("act", e)

            @block.vector
            def _(e):
                run("dve", e)

            @block.gpsimd
            def _(e):
                run("pool", e)
        self.est_ns = t
        return {e: len(self.ops[e]) for e in ENGS}


CF = {}
_o = 0
for _n, _w in (("ident", 128), ("tri", 128), ("triB", 128), ("ones", 128), ("onesB", 128),
               ("cosP", 128), ("sinP", 128), ("cosS", 8), ("sinS", 8), ("blkcol", 16)):
    CF[_n] = (_o, _w)
    _o += _w
NCF = _o
CB = {}
_o = 0
for _n, _w in (("ident", 128), ("mcur", 256), ("mprev", 256),
               ("wcache", 256), ("wnew", 256), ("blkrow", 2048), ("dmask4", 512), ("dmaskB4", 512), ("m01", 512)):
    CB[_n] = (_o, _w)
    _o += _w
NCB = _o


def make_consts():
    cf = np.zeros((128, NCF), np.float32)
    cb = np.zeros((128, NCB), np.float32)
    j = np.arange(128)[:, None]
    i = np.arange(128)[None, :]
    blk = (j // 8) == (i // 8)

    def setf(n, v):
        o, w = CF[n]
        cf[:, o:o + w] = np.asarray(v, np.float32).reshape(128, w)

    def setb(n, v):
        o, w = CB[n]
        cb[:, o:o + w] = np.asarray(v, np.float32).reshape(128, w)

    setf("ident", np.eye(128))
    setf("tri", (j <= i))
    setf("triB", (j <= i) & blk)
    setf("ones", np.ones((128, 128)))
    setf("onesB", blk)
    inv = 500000.0 ** (-np.arange(0, 16, 2, dtype=np.float32) / 16.0)
    pos = (np.arange(16)[None, :, None] * 128 + np.arange(128)[:, None, None]).astype(np.float32)
    ang = (pos * inv[None, None, :].astype(np.float32)).astype(np.float32)
    setf("cosP", np.cos(ang))
    setf("sinP", np.sin(ang))
    poss = (8192 + (np.arange(128) % 8)).astype(np.float32)[:, None]
    angs = (poss * inv[None, :].astype(np.float32)).astype(np.float32)
    setf("cosS", np.cos(angs))
    setf("sinS", np.sin(angs))
    setf("blkcol", (np.arange(128)[:, None] // 8) == np.arange(16)[None, :])

    setb("ident", np.eye(128))
    mc = np.where(j <= i, 0.0, NEG)
    mp = np.where(j >= i, 0.0, NEG)
    setb("m01", np.concatenate([(j <= i), (j <= i), (j >= i), (j >= i)], 1).astype(np.float32))
    setb("mcur", np.concatenate([mc, mc], 1))
    setb("mprev", np.concatenate([mp, mp], 1))
    setb("dmask4", np.tile(np.where(j <= i, 0.0, NEG), (1, 4)))
    setb("dmaskB4", np.tile(np.where((j <= i) & blk, 0.0, NEG), (1, 4)))
    W = np.zeros((128, 16, 2, 8), np.float32)
    for d in (1, 4, 16):
        for t in range(8):
            for jj in range(129):
                row = 2048 + t - d * jj
                if 0 <= row < 2048:
                    W[row // 16, row % 16, :, t] += 1.0
    setb("wcache", W)
    Wn = np.zeros((128, 16, 2, 8), np.float32)
    for b in range(16):
        for t in range(8):
            for tp in range(t + 1):
                dlt = t - tp
                w = 1.0 + (1.0 if dlt % 4 == 0 else 0.0) + (1.0 if dlt == 0 else 0.0)
                Wn[b * 8 + tp, b, :, t] = w
    setb("wnew", Wn)
    br = np.zeros((128, 16, 128), np.float32)
    for b in range(16):
        br[:, b, b * 8:(b + 1) * 8] = 1.0
    setb("blkrow", br)
    return cf, cb.astype(ml_dtypes.bfloat16)


def build(NP=2, NSB=16, stop_after=None, debug_out=False):
    nc = bass.Bass("TRN2", target_bir_lowering=False)
    NTOK = NP * S
    NST = NSB * 8

    def din(name, shape, dt=F32):
        return nc.dram_tensor(name, list(shape), dt, kind="ExternalInput")

    def dout(name, shape, dt=F32):
        return nc.dram_tensor(name, list(shape), dt, kind="ExternalOutput")

    x_prompt = din("x_prompt", [NTOK, D])
    x_sample = din("x_sample", [NST, D])
    cache_k = din("cache_k", [NSB, S * 256])
    cache_v = din("cache_v", [NSB, S * 256])
    state_conv = din("state_conv", [NSB * 3, 1024])
    state_ssm = din("state_ssm", [NSB * 512, 128])
    w_in = din("w_in", [D, 2568])
    w_out = din("w_out", [D, D])
    conv_w = din("conv_w", [4, 1024])
    conv_b = din("conv_b", [1, 1024])
    dt_bias = din("dt_bias", [1, 8])
    a_log = din("a_log", [1, 8])
    d_skip = din("d_skip", [1, 8])
    ssm_norm = din("ssm_norm", [1, 512])
    norm_mix = din("norm_mix", [1, D])
    norm_mlp = din("norm_mlp", [1, D])
    w_up = din("w_up", [D, 4096])
    w_down = din("w_down", [4096, D])
    norm_final = din("norm_final", [1, D])
    cf_d = din("cf", [128, NCF])
    cb_d = din("cb", [128, NCB], BF16)

    y_prompt = dout("y_prompt", [NTOK, D])
    y_sample = dout("y_sample", [NST, D])
    k_prompt = dout("k_prompt", [NTOK, 256])
    v_prompt = dout("v_prompt", [NTOK, 256])
    conv_prompt = dout("conv_prompt", [NP * 3, 1024])
    ssm_prompt = dout("ssm_prompt", [NP * 512, 128])
    k_sample = dout("k_sample", [NSB, S * 256])
    v_sample = dout("v_sample", [NSB, S * 256])
    conv_sample = dout("conv_sample", [NSB * 3, 1024])
    ssm_sample = dout("ssm_sample", [NSB * 512, 128])
    hbuf = nc.dram_tensor("hbuf", [NTOK + NST, D], F32, kind="Internal")
    dbg = dout("dbg", [128, 8 * 2048 + 16], F32) if debug_out else None

    P = Prog(nc, reorder=not os.environ.get("NOREORDER"))
    st = contextlib.ExitStack()

    def T(name, shape, dt):
        h = st.enter_context(nc.sbuf_tensor("sb_" + name, list(shape), dt))
        return h[:]

    def sap(t, p0, npart, off, dims):
        F = t.ap[0][0]
        return bass.AP(t.tensor, t.offset + p0 * F + off, [[F, npart]] + [list(d) for d in dims])

    ARENA_BYTES = 168 * 1024
    arena_h = st.enter_context(nc.sbuf_tensor("sb_arena", [128, ARENA_BYTES // 2], BF16))
    arena_f = arena_h.bitcast(F32)
    arena_pos = [0]

    arena_mark = [0]

    def arena_reset():
        arena_pos[0] = arena_mark[0]

    def AT(shape, dt):
        n = 1
        for v in shape[1:]:
            n *= v
        esz = 4 if dt == F32 else 2
        nbytes = (n * esz + 31) // 32 * 32
        b0 = arena_pos[0]
        arena_pos[0] += nbytes
        assert arena_pos[0] <= ARENA_BYTES, ("arena overflow", arena_pos[0])
        h = arena_f if dt == F32 else arena_h
        v = h[:, b0 // esz:b0 // esz + n]
        if len(shape) == 3:
            v = v.rearrange("p (a b) -> p a b", a=shape[1])
        elif len(shape) == 4:
            v = v.rearrange("p (a b c) -> p a b c", a=shape[1], b=shape[2])
        elif len(shape) == 5:
            v = v.rearrange("p (a b c d) -> p a b c d", a=shape[1], b=shape[2], c=shape[3])
        return v

    def dap(t, off, dims):
        return bass.AP(t, off, [list(d) for d in dims])

    ps_h = [st.enter_context(nc.psum_tensor("ps%d" % k, [128, 512], F32)) for k in range(8)]
    ps = [p[:] for p in ps_h]
    psb = [p.bitcast(BF16)[:] for p in ps_h]

    cf = T("cf", [128, NCF], F32)
    cb = T("cb", [128, NCB], BF16)
    P.dma("sp", cf[:], cf_d.ap(), w=["cf"])
    P.dma("sp", cb[:], cb_d.ap(), w=["cb"])

    def CFs(n, p0=0, npart=128):
        o, w = CF[n]
        return cf[p0:p0 + npart, o:o + w]

    def CBs(n, p0=0, npart=128):
        o, w = CB[n]
        return cb[p0:p0 + npart, o:o + w]

    gmix = T("gmix", [128, D], F32)
    P.dma("sp", gmix[:], dap(norm_mix, 0, [[0, 128], [1, D]]), w=["gmix"])
    epsb = T("epsb", [128, 1], F32)
    P.op("pool", I.memset(epsb[:], EPS), w=["epsb"])
    oneb = T("oneb", [128, 1], F32)
    P.op("pool", I.memset(oneb[:], 1.0), w=["oneb"])
    cw = T("cw", [128, 4, 8], F32)
    cbias = T("cbias", [128, 8], F32)
    dtb = T("dtb", [128, 8], F32)
    aneg = T("aneg", [128, 8], F32)
    dsk = T("dsk", [128, 8], F32)
    ssn = T("ssn", [128, 512], F32)
    for tap in range(4):
        P.dma("act", cw[:, tap, :], dap(conv_w, tap * 1024, [[1, 128], [128, 8]]), w=["cw"], allow_slow_non_contiguous=True)
    P.dma("act", cbias[:], dap(conv_b, 0, [[1, 128], [128, 8]]), w=["cbias"], allow_slow_non_contiguous=True)
    P.dma("sp", dtb[:], dap(dt_bias, 0, [[0, 128], [1, 8]]), w=["dtb"])
    P.dma("sp", aneg[:], dap(a_log, 0, [[0, 128], [1, 8]]), w=["aneg"])
    P.dma("sp", dsk[:], dap(d_skip, 0, [[0, 128], [1, 8]]), w=["dsk"])
    P.dma("sp", ssn[:], dap(ssm_norm, 0, [[0, 128], [1, 512]]), w=["ssn"])
    P.op("act", I.activation(out=aneg[:], in_=aneg[:], func=AF.Exp), r=["aneg"], w=["aneg"])
    P.op("dve", I.tensor_scalar(out=aneg[:], in0=aneg[:], scalar1=-1.0, scalar2=None, op0=ALU.mult), r=["aneg"], w=["aneg"])
    xres = [T("xres0", [128, D], F32)] * 2

    win = AT([128, 8, 2568], BF16)
    wout = AT([128, 8, D], BF16)
    mixT = AT([128, 4, S], BF16)
    arena_mark[0] = arena_pos[0]
    for k in range(8):
        P.dma("pool", win[:, k, 0:1024], w_in.ap()[k * 128:(k + 1) * 128, 0:1024], w=["winA%d" % k])
    for k in range(8):
        P.dma("pool", win[:, k, 1024:2568], w_in.ap()[k * 128:(k + 1) * 128, 1024:2568], w=["winB%d" % k])
    for k in range(8):
        P.dma("pool", wout[:, k, :], w_out.ap()[k * 128:(k + 1) * 128, :], w=["wout%d" % k])
    WINA = ["winA%d" % k for k in range(8)]
    WIN = ["winB%d" % k for k in range(8)]
    WOUT = ["wout%d" % k for k in range(8)]


    copy_jobs = []
    if not os.environ.get("NOCOPY"):
        for b in range(NSB):
            copy_jobs.append((k_sample, cache_k, b))
            copy_jobs.append((v_sample, cache_v, b))

    def cache_copy_some(n, after=()):
        for _ in range(n):
            if copy_jobs:
                dst, src, b = copy_jobs.pop(0)
                P.dma("act", dst.ap()[b:b + 1, 0:2040 * 256], src.ap()[b:b + 1, 8 * 256:S * 256], after=after)

    xs = [T("xs%d" % k, [128, D], F32) for k in range(2)]
    xn = [T("xn%d" % k, [128, D], BF16) for k in range(2)]
    ssq = [T("ssq%d" % k, [128, 4], F32) for k in range(2)]
    xT = [T("xT%d" % k, [128, 8, 128], BF16) for k in range(2)]
    def alloc_pass1():
        arena_reset()
        d_ = {}
        d_["qT"] = AT([128, 4, S], BF16)
        d_["kT"] = AT([128, 2, S], BF16)
        d_["V3"] = AT([128, 3, 16, 4, 65], BF16)
        d_["acc"] = AT([128, 2, S], F32)
        d_["ET"] = [AT([128, 512], BF16) for k in range(3)]
        d_["qk"] = [AT([128, 12, 64], F32) for k in range(2)]
        d_["rt"] = [AT([128, 4, 12, 8], F32) for k in range(2)]
        d_["vst"] = [AT([128, 256], F32) for k in range(2)]
        d_["qkb"] = [AT([128, 768], BF16) for k in range(2)]
        return d_
    A1 = alloc_pass1()
    qT, kT, V3, acc, ET, qk, rt, vst, qkb = (A1[k] for k in ("qT", "kT", "V3", "acc", "ET", "qk", "rt", "vst", "qkb"))
    V3ALL = ["V3_%d_%d" % (a, b) for a in range(3) for b in range(16)]

    def rstd_ops(sq, sqn, scale, c0=0, n=1):
        P.op("act", I.activation(out=sq[:, n + c0:2 * n + c0], in_=sq[:, c0:c0 + n], func=AF.Ln, scale=scale, bias=epsb[:, 0:1]),
             r=[sqn, "epsb"], w=[sqn])
        P.op("act", I.activation(out=sq[:, 2 * n + c0:3 * n + c0], in_=sq[:, n + c0:2 * n + c0], func=AF.Exp, scale=-0.5),
             r=[sqn], w=[sqn])

    def rmsnorm_to_xT(src_ap_rows, sl, gain, tagx, dst3=None, dst_tag=None, pb=None):
        P.dma("sp", xs[sl][:], src_ap_rows, w=["xs%d" % sl])
        P.op("pool", I.memset(ssq[sl][:], 0.0), w=["ssq%d" % sl])
        P.op("act", I.activation(out=xn[sl][:], in_=xs[sl][:], func=AF.Square,
                                           accum_out=ssq[sl][:, 0:1]),
             r=["xs%d" % sl, "ssq%d" % sl], w=["xn%d" % sl, "ssq%d" % sl])
        rstd_ops(ssq[sl], "ssq%d" % sl, 1.0 / D)
        P.op("dve", I.scalar_tensor_tensor(out=xn[sl][:], in0=xs[sl][:], scalar=ssq[sl][:, 2:3],
                                                     in1=gain[:], op0=ALU.mult, op1=ALU.mult),
             r=["xs%d" % sl, "ssq%d" % sl, tagx], w=["xn%d" % sl])
        if pb is None:
            pb = 4 * sl
        if dst3 is None:
            dst3, dst_tag = xT[sl], "xT%d" % sl
        for k in range(8):
            P.op("pe", I.transpose(out=psb[pb][:, k * 128:(k + 1) * 128],
                                                  in_=xn[sl][:, k * 128:(k + 1) * 128], identity=CBs("ident")),
                 r=["xn%d" % sl, "cb"], w=["ps%d" % pb])
        return P.op("act", I.copy(out=dst3, in_=psb[pb][:, 0:1024].rearrange("p (k t) -> p k t", k=8)),
                    r=["ps%d" % pb], w=[dst_tag])


    def alloc_pass2(sample=False):
        arena_reset()
        d_ = {}
        if not sample:
            d_["xT5"] = [AT([128, 8, 512], BF16) for k in range(2)]
        d_["pre"] = [AT([128, 520], F32) for k in range(2)]
        d_["halo"] = AT([128, 8, 4], F32)
        d_["cacc"] = [AT([128, 512], F32) for k in range(2)]
        d_["xbcT"] = [AT([128, 8, 512], BF16) for k in range(1 if sample else 2)]
        d_["tp"] = 0
        d_["tm"] = AT([128, 4, 768], BF16)
        nb_ = 1 if sample else 2
        d_["nbuf"] = nb_
        d_["sm"] = [AT([128, 64], F32) for k in range(nb_)]
        d_["dtt"] = [AT([128, 8], F32) for k in range(nb_)]
        d_["aa"] = [AT([128, 8], F32) for k in range(nb_)]
        d_["abc"] = [AT([128, 8, 128], F32) for k in range(nb_)]
        d_["decT"] = [AT([128, 8, 128], F32) for k in range(nb_)]
        d_["MT"] = [AT([128, 8, 128], BF16) for k in range(nb_)]
        d_["y0"] = [AT([128, 512], F32) for k in range(nb_)]
        d_["y1"] = [AT([128, 512], F32) for k in range(nb_)]
        d_["y2"] = [AT([128, 512], F32)]
        d_["siluz"] = [AT([128, 512], F32)]
        d_["ygn"] = [AT([128, 512], BF16) for k in range(nb_)]
        d_["xw"] = [AT([128, 512], BF16) for k in range(nb_)]
        d_["hT"] = AT([128, 512], F32)
        d_["hTb"] = AT([128, 512], BF16)
        d_["hout"] = [AT([128, D], F32)]
        d_["xtok"] = d_["decT"][-1].rearrange("p a b -> p (a b)")
        d_["ss2"] = [AT([128, 8], F32) for k in range(nb_)]
        d_["stT"] = d_["abc"][-1][:, 0:4, :]
        d_["mixS"] = [AT([128, 4, 128], BF16) for k in range(2)]
        return d_

    def bc8(ap8, n):
        return bass.AP(ap8.tensor, ap8.offset, [list(ap8.ap[0]), [1, 8], [0, n]])

    def ssd_chunk(A, sub, tokrow0, mix_cols, first_chunk, tri_n, ones_n, dmask_n, xsrc_rows, hb_row0, prompt, yoff_emit=None):
        bi = sub % A["nbuf"]
        tp_ = A["tp"]
        xT5 = A["xT5"][tp_] if isinstance(A["xT5"], list) else A["xT5"]
        xbcT = A["xbcT"][tp_]
        tm, hT, hTb = (A[k] for k in ("tm", "hT", "hTb"))
        sm, dtt, aa, abc, decT, MT, y0, y1, y2, siluz, ygn, xw, ss2 = (A[k][bi % len(A[k])] for k in (
            "sm", "dtt", "aa", "abc", "decT", "MT", "y0", "y1", "y2", "siluz", "ygn", "xw", "ss2"))
        c0, c1 = sub * 128, (sub + 1) * 128
        XT = ["xT5_%d_%d" % (tp_, sub)]
        XB = ["xbcT%d_%d_%d" % (tp_, ch, sub) for ch in range(8)]
        for ch in range(6):
            P.op("pe", I.transpose(out=psb[2][:, ch * 128:(ch + 1) * 128], in_=xbcT[:, ch, c0:c1], identity=CBs("ident")),
                 r=[XB[ch], "cb"], w=["ps2"])
        P.op("act", I.copy(out=tm[:, sub, :], in_=psb[2][:, 0:768]), r=["ps2"], w=["tm%d" % sub])
        TM = ["tm%d" % sub]
        for k in range(8):
            P.op("pe", I.matmul(ps[6][:, 256:264], lhsT=xT5[:, k, c0:c1], rhs=win[:, k, 2560:2568], start=(k == 0), stop=(k == 7)),
                 r=XT + [WIN[k]], w=["ps6"])
        P.op("dve", I.tensor_tensor(out=sm[:, 0:8], in0=ps[6][:, 256:264], in1=dtb[:], op=ALU.add), r=["ps6", "dtb"], w=["sm_%d" % bi])
        P.op("act", I.activation(out=sm[:, 8:16], in_=sm[:, 0:8], func=AF.Exp), r=["sm_%d" % bi], w=["sm_%d" % bi])
        P.op("act", I.activation(out=dtt[:], in_=sm[:, 8:16], func=AF.Ln, bias=oneb[:, 0:1]), r=["sm_%d" % bi, "oneb"], w=["dtt_%d" % bi])
        P.op("dve", I.tensor_tensor(out=aa[:], in0=dtt[:], in1=aneg[:], op=ALU.mult), r=["dtt_%d" % bi, "aneg"], w=["aa_%d" % bi])
        P.op("pe", I.matmul(ps[6][:, 264:272], lhsT=CFs(tri_n), rhs=aa[:], start=True, stop=True), r=["aa_%d" % bi, "cf"], w=["ps6"])
        P.op("pe", I.matmul(ps[6][:, 272:280], lhsT=CFs(ones_n), rhs=aa[:], start=True, stop=True), r=["aa_%d" % bi, "cf"], w=["ps6"])
        P.op("dve", I.tensor_scalar(out=sm[:, 16:24], in0=ps[6][:, 264:272], scalar1=-1.0, scalar2=None, op0=ALU.mult),
             r=["ps6"], w=["sm_%d" % bi])
        P.op("act", I.activation(out=sm[:, 24:32], in_=ps[6][:, 264:272], func=AF.Exp), r=["ps6"], w=["sm_%d" % bi])
        P.op("dve", I.tensor_tensor(out=sm[:, 32:40], in0=ps[6][:, 272:280], in1=sm[:, 16:24], op=ALU.add),
             r=["ps6", "sm_%d" % bi], w=["sm_%d" % bi])
        P.op("act", I.activation(out=sm[:, 40:48], in_=sm[:, 32:40], func=AF.Exp), r=["sm_%d" % bi], w=["sm_%d" % bi])
        P.op("dve", I.tensor_tensor(out=sm[:, 48:56], in0=sm[:, 40:48], in1=dtt[:], op=ALU.mult), r=["sm_%d" % bi, "dtt_%d" % bi], w=["sm_%d" % bi])
        P.op("act", I.activation(out=sm[:, 56:64], in_=ps[6][:, 272:280], func=AF.Exp), r=["ps6"], w=["sm_%d" % bi])
        o_t, _w = CF[tri_n]
        tri_b = bass.AP(cf.tensor, cf.offset + o_t, [list(cf.ap[0]), [0, 8], [1, 128]])
        P.op("dve", I.tensor_tensor(out=abc[:], in0=tri_b, in1=bc8(aa, 128), op=ALU.mult), r=["aa_%d" % bi, "cf"], w=["abc_%d" % bi])
        dm4 = "dmask4" if dmask_n == "dmaskF" else "dmaskB4"
        for hb in range(2):
            bank = 4 + hb
            P.op("pe", I.matmul(ps[bank][:, 0:512], lhsT=CFs("ones"), rhs=abc[:, hb * 4:(hb + 1) * 4, :].rearrange("p a b -> p (a b)"),
                                start=True, stop=False), r=["abc_%d" % bi, "cf"], w=["ps%d" % bank])
            P.op("pe", I.matmul(ps[bank][:, 0:512], lhsT=CBs("ident"), rhs=CBs(dm4), start=False, stop=True),
                 r=["cb"], w=["ps%d" % bank])
        for h in range(8):
            bank = 4 + h // 4
            col = (h % 4) * 128
            P.op("act", I.activation(out=decT[:, h, :], in_=ps[bank][:, col:col + 128], func=AF.Exp,
                                     bias=sm[:, 16 + h:17 + h]), r=["ps%d" % bank, "sm_%d" % bi], w=["decT_%d" % bi])
        for g in range(2):
            P.op("pe", I.matmul(ps[6][:, g * 128:(g + 1) * 128], lhsT=xbcT[:, 4 + g, c0:c1], rhs=xbcT[:, 6 + g, c0:c1],
                                start=True, stop=True), r=[XB[4 + g], XB[6 + g]], w=["ps6"])
        for h in range(8):
            g = h // 4
            P.op("dve", I.scalar_tensor_tensor(out=MT[:, h, :], in0=decT[:, h, :], scalar=dtt[:, h:h + 1],
                                               in1=ps[6][:, g * 128:(g + 1) * 128], op0=ALU.mult, op1=ALU.mult),
                 r=["decT_%d" % bi, "dtt_%d" % bi, "ps6"], w=["MT_%d" % bi])
        for h in range(8):
            P.op("pe", I.matmul(ps[7][:, h * 64:(h + 1) * 64], lhsT=MT[:, h, :], rhs=tm[:, sub, h * 64:(h + 1) * 64],
                                start=True, stop=True), r=["MT_%d" % bi] + TM, w=["ps7"])
        if yoff_emit is None:
            for g in range(2):
                P.op("pe", I.matmul(ps[0][:, g * 256:(g + 1) * 256], lhsT=xbcT[:, 6 + g, c0:c1], rhs=hTb[:, g * 256:(g + 1) * 256],
                                    start=True, stop=True), r=[XB[6 + g], "hTb"], w=["ps0"])
        else:
            P.op("pool", I.tensor_tensor(out=xw[:].rearrange("p (h d) -> p h d", h=8), in0=tm[:, sub, 0:512].rearrange("p (h d) -> p h d", h=8),
                                         in1=bc8(sm[:, 48:56], 64), op=ALU.mult), r=TM + ["sm_%d" % bi], w=["xw_%d" % bi])
            yoff_emit()
        v3 = lambda ap: ap.rearrange("p (h d) -> p h d", h=8)
        ysrc = [(ps[0][:, 0:256], "ps0"), (ps[0][:, 256:512], "ps0")] if yoff_emit is None else \
               [(ps[0][:, 0:256], "ps0"), (ps[6][:, 256:512], "ps6")]
        for g in range(2):
            ein = bass.AP(sm.tensor, sm.offset + 24 + 4 * g, [list(sm.ap[0]), [1, 4], [0, 64]])
            P.op("dve", I.tensor_tensor(out=y0[:, g * 256:(g + 1) * 256].rearrange("p (h d) -> p h d", h=4),
                                        in0=ysrc[g][0].rearrange("p (h d) -> p h d", h=4), in1=ein, op=ALU.mult),
                 r=[ysrc[g][1], "sm_%d" % bi], w=["y0_%d" % bi])
        P.op("pool", I.tensor_tensor(out=v3(y1[:]), in0=v3(tm[:, sub, 0:512]), in1=bc8(dsk[:], 64), op=ALU.mult),
             r=TM + ["dsk"], w=["y1_%d" % bi])
        P.op("pool", I.tensor_tensor(out=y1[:], in0=y1[:], in1=y0[:], op=ALU.add), r=["y0_%d" % bi, "y1_%d" % bi], w=["y1_%d" % bi])
        P.op("dve", I.tensor_tensor(out=y2[:], in0=ps[7][:, 0:512], in1=y1[:], op=ALU.add), r=["ps7", "y1_%d" % bi], w=["y2_0"])
        ssd_gate_and_out(A, sub, tokrow0, xsrc_rows, hb_row0)
        if not prompt:
            return
        P.op("pool", I.tensor_tensor(out=v3(xw[:]), in0=v3(tm[:, sub, 0:512]), in1=bc8(sm[:, 48:56], 64), op=ALU.mult),
             r=TM + ["sm_%d" % bi], w=["xw_%d" % bi])
        P.op("pe", I.matmul(ps[1][:, 0:256], lhsT=tm[:, sub, 512:640], rhs=xw[:, 0:256], start=True, stop=True),
             r=TM + ["xw_%d" % bi], w=["ps1"])
        P.op("pe", I.matmul(ps[1][:, 256:512], lhsT=tm[:, sub, 640:768], rhs=xw[:, 256:512], start=True, stop=True),
             r=TM + ["xw_%d" % bi], w=["ps1"])
        P.op("dve", I.tensor_tensor(out=v3(hT[:]), in0=v3(hT[:]), in1=bc8(sm[:, 56:64], 64), op=ALU.mult),
             r=["hT", "sm_%d" % bi], w=["hT"])
        P.op("dve", I.tensor_tensor(out=hT[:], in0=hT[:], in1=ps[1][:, 0:512], op=ALU.add), r=["hT", "ps1"], w=["hT"])
        P.op("act", I.copy(out=hTb[:], in_=hT[:]), r=["hT"], w=["hTb"])

    def ssd_gate_and_out(A, sub, tokrow0, xsrc_rows, hb_row0):
        bi = sub % A["nbuf"]
        tp_ = A["tp"]
        xT5 = A["xT5"][tp_] if isinstance(A["xT5"], list) else A["xT5"]
        hout = A["hout"]
        y0, y1, y2, siluz, ygn, ss2 = (A[k][bi % len(A[k])] for k in ("y0", "y1", "y2", "siluz", "ygn", "ss2"))
        c0, c1 = sub * 128, (sub + 1) * 128
        XT = ["xT5_%d" % sub]
        for k in range(8):
            P.op("pe", I.matmul(ps[1][:, 0:512], lhsT=xT5[:, k, c0:c1], rhs=win[:, k, 1024:1536], start=(k == 0), stop=(k == 7)),
                 r=XT + [WIN[k]], w=["ps1"])
        P.op("act", I.activation(out=siluz[:], in_=ps[1][:, 0:512], func=AF.Silu), r=["ps1"], w=["siluz_0"])
        P.op("dve", I.tensor_tensor(out=y0[:], in0=y2[:], in1=siluz[:], op=ALU.mult), r=["y2_0", "siluz_0"], w=["y0_%d" % bi])
        P.op("pool", I.memset(ss2[:], 0.0), w=["ss2_%d" % bi])
        for g in range(2):
            P.op("act", I.activation(out=y1[:, g * 256:(g + 1) * 256], in_=y0[:, g * 256:(g + 1) * 256], func=AF.Square,
                                     accum_out=ss2[:, g:g + 1]), r=["y0_%d" % bi, "ss2_%d" % bi], w=["y1_%d" % bi, "ss2_%d" % bi])
        rstd_ops(ss2, "ss2_%d" % bi, 1.0 / 256, 0, 2)
        for g in range(2):
            P.op("dve", I.scalar_tensor_tensor(out=ygn[:, g * 256:(g + 1) * 256], in0=y0[:, g * 256:(g + 1) * 256],
                                               scalar=ss2[:, 4 + g:5 + g], in1=ssn[:, g * 256:(g + 1) * 256],
                                               op0=ALU.mult, op1=ALU.mult), r=["y0_%d" % bi, "ss2_%d" % bi, "ssn"], w=["ygn_%d" % bi])
        YB = 3
        for k in range(4):
            P.op("pe", I.transpose(out=psb[YB][:, k * 128:(k + 1) * 128], in_=ygn[:, k * 128:(k + 1) * 128], identity=CBs("ident")),
                 r=["ygn_%d" % bi, "cb"], w=["ps%d" % YB])
        sl = sub % 2
        hs_ = 0
        mixS = A["mixS"][sl]
        P.op("act", I.copy(out=mixS[:], in_=psb[YB][:, 0:512].rearrange("p (k t) -> p k t", k=4)),
             r=["ps%d" % YB], w=["mixS%d" % sl])
        P.dma("sp", xres[sl][:], xsrc_rows, w=["xres"])
        for half in range(2):
            ob = (7, 0)[half]
            for k in range(8):
                lh = mixT[:, k, tokrow0:tokrow0 + 128] if k < 4 else mixS[:, k - 4, :]
                P.op("pe", I.matmul(ps[ob][:, 0:512], lhsT=lh,
                                    rhs=wout[:, k, half * 512:(half + 1) * 512], start=(k == 0), stop=(k == 7)),
                     r=["mixA", "mixS%d" % sl, WOUT[k]], w=["ps%d" % ob])
            P.op("dve", I.tensor_tensor(out=hout[hs_][:, half * 512:(half + 1) * 512], in0=ps[ob][:, 0:512],
                                        in1=xres[sl][:, half * 512:(half + 1) * 512], op=ALU.add),
                 r=["ps%d" % ob, "xres"], w=["hout%d" % hs_])
        P.dma("sp", hbuf.ap()[hb_row0:hb_row0 + 128, :], hout[hs_][:], r=["hout%d" % hs_], w=["hbuf_%d" % hb_row0])

    def ssd_pass(A, prompt, s):
        pre, halo, cacc, hT, hTb, xtok, stT = (A[k] for k in ("pre", "halo", "cacc", "hT", "hTb", "xtok", "stT"))
        P.op("pool", I.memset(halo[:].rearrange("p a b -> p (a b)"), 0.0), w=["halo"])
        P.op("pool", I.memset(hT[:], 0.0), w=["hT"])
        P.op("pool", I.memset(hTb[:], 0.0), w=["hTb"])
        for Tt in range(4):
            tb = s * S + Tt * 512
            tp_ = Tt % 2
            A["tp"] = tp_
            xT5, xbcT = A["xT5"][tp_], A["xbcT"][tp_]
            for sub in range(4):
                rmsnorm_to_xT(x_prompt.ap()[tb + sub * 128:tb + (sub + 1) * 128, :], sub % 2, gmix, "gmix",
                              dst3=xT5[:, :, sub * 128:(sub + 1) * 128], dst_tag="xT5_%d_%d" % (tp_, sub), pb=2)
            XTall = ["xT5_%d_%d" % (tp_, x) for x in range(4)]
            XO = 0
            for ch in range(8):
                u = ch % 2
                for k in range(8):
                    P.op("pe", I.matmul(ps[XO + u][:, 0:512], lhsT=win[:, k, 1536 + ch * 128:1536 + (ch + 1) * 128],
                                        rhs=xT5[:, k, :], start=(k == 0), stop=(k == 7)), r=XTall + [WIN[k]], w=["ps%d" % (XO + u)])
                P.op("act", I.copy(out=pre[u][:, 3:515], in_=ps[XO + u][:, 0:512]), r=["ps%d" % (XO + u)], w=["pre%d" % u])
                P.op("pool", I.tensor_copy(out=pre[u][:, 0:3], in_=halo[:, ch, 0:3]), r=["halo"], w=["pre%d" % u])
                P.op("dve", I.tensor_scalar(out=cacc[u][:], in0=pre[u][:, 3:515], scalar1=cw[:, 3, ch:ch + 1], scalar2=cbias[:, ch:ch + 1],
                                            op0=ALU.mult, op1=ALU.add), r=["pre%d" % u, "cw", "cbias"], w=["cacc%d" % u])
                for tap in (2, 1, 0):
                    P.op("dve", I.scalar_tensor_tensor(out=cacc[u][:], in0=pre[u][:, tap:tap + 512], scalar=cw[:, tap, ch:ch + 1],
                                                       in1=cacc[u][:], op0=ALU.mult, op1=ALU.add),
                         r=["pre%d" % u, "cw", "cacc%d" % u], w=["cacc%d" % u])
                P.op("act", I.activation(out=xbcT[:, ch, :], in_=cacc[u][:], func=AF.Silu), r=["cacc%d" % u],
                     w=["xbcT%d_%d_%d" % (tp_, ch, x) for x in range(4)])
                P.op("pool", I.tensor_copy(out=halo[:, ch, 0:3], in_=pre[u][:, 512:515]), r=["pre%d" % u], w=["halo"])
            for sub in range(4):
                tr = Tt * 512 + sub * 128
                ssd_chunk(A, sub, tr, None, (Tt == 0 and sub == 0), "tri", "ones", "dmaskF",
                          x_prompt.ap()[tb + sub * 128:tb + (sub + 1) * 128, :], tb + sub * 128, True)
            if Tt == 3:
                for half in range(2):
                    for k in range(8):
                        P.op("pe", I.matmul(ps[4 + half][:, 0:512], lhsT=xT5[:, k, 384:512],
                                            rhs=win[:, k, 1536 + half * 512:1536 + (half + 1) * 512], start=(k == 0), stop=(k == 7)),
                             r=["xT5_%d_3" % tp_, WIN[k]], w=["ps%d" % (4 + half)])
                    P.op("act", I.copy(out=xtok[:, half * 512:(half + 1) * 512], in_=ps[4 + half][:, 0:512]),
                         r=["ps%d" % (4 + half)], w=["decT_1"])
                P.dma("sp", conv_prompt.ap()[s * 3:s * 3 + 3, :], xtok[125:128, :], r=["decT_1"])
        for q in range(4):
            P.op("pe", I.transpose(out=ps[7][:, q * 128:(q + 1) * 128], in_=hT[:, q * 128:(q + 1) * 128], identity=CFs("ident")),
                 r=["hT", "cf"], w=["ps7"])
        P.op("act", I.copy(out=stT[:].rearrange("p q n -> p (q n)"), in_=ps[7][:, 0:512]), r=["ps7"], w=["abc_1"])
        P.dma("sp", dap(ssm_prompt, s * 512 * 128, [[128, 128], [128 * 128, 4], [1, 128]]), stT[:], r=["abc_1"])

    def rope_qk(sl, cos_ap, sin_ap, pq, pkv):
        P.op("act", I.mul(out=qk[sl][:, 0:8, :].rearrange("p h d -> p (h d)"), in_=ps[pq][:, 0:512],
                                    mul=0.125), r=["ps%d" % pq], w=["qk%d" % sl])
        P.op("act", I.copy(out=qk[sl][:, 8:12, :].rearrange("p h d -> p (h d)"), in_=ps[pkv][:, 0:256]),
             r=["ps%d" % pkv], w=["qk%d" % sl])
        x1 = qk[sl][:, :, 0:8]
        x2 = qk[sl][:, :, 8:16]
        def bc(ap2):
            return bass.AP(ap2.tensor, ap2.offset, [list(ap2.ap[0]), [0, 12], [1, 8]])
        cb_, sb_ = bc(cos_ap), bc(sin_ap)
        R = ["qk%d" % sl, "cf"]
        P.op("dve", I.tensor_tensor(out=rt[sl][:, 0], in0=x1, in1=cb_, op=ALU.mult), r=R, w=["rt%d" % sl])
        P.op("dve", I.tensor_tensor(out=rt[sl][:, 1], in0=x2, in1=sb_, op=ALU.mult), r=R, w=["rt%d" % sl])
        P.op("dve", I.tensor_tensor(out=rt[sl][:, 2], in0=x2, in1=cb_, op=ALU.mult), r=R, w=["rt%d" % sl])
        P.op("dve", I.tensor_tensor(out=rt[sl][:, 3], in0=x1, in1=sb_, op=ALU.mult), r=R, w=["rt%d" % sl])
        P.op("dve", I.tensor_tensor(out=x1, in0=rt[sl][:, 0], in1=rt[sl][:, 1], op=ALU.subtract),
             r=["rt%d" % sl], w=["qk%d" % sl])
        P.op("dve", I.tensor_tensor(out=x2, in0=rt[sl][:, 2], in1=rt[sl][:, 3], op=ALU.add),
             r=["rt%d" % sl], w=["qk%d" % sl])

    def qk_to_bf16(sl):
        for c in range(2):
            src = sap(qk[sl], 0, 128, c * 256, [[64, 2], [128, 2], [1, 64]])
            dst = qkb[sl][:, c * 256:(c + 1) * 256].rearrange("p (g k d) -> p g k d", g=2, k=2)
            P.op("pool", I.tensor_copy(out=dst, in_=src),
                 r=["qk%d" % sl], w=["qkb%d" % sl])
        P.op("pool", I.tensor_copy(out=qkb[sl][:, 512:768],
                                             in_=qk[sl][:, 8:12, :].rearrange("p h d -> p (h d)")),
             r=["qk%d" % sl], w=["qkb%d" % sl])

    for s in range(NP if stop_after != "sample" else 0):
        P.op("pool", I.memset(V3[:].rearrange("p a b c d -> p (a b c d)"), 1.0), w=V3ALL)
        STEP = int(os.environ.get("STEP", "99"))
        for i in range(int(os.environ.get("LIMI", "16"))):
            sl = i % 2
            pb = 4 * sl
            tok0 = s * S + i * 128
            last_ = rmsnorm_to_xT(x_prompt.ap()[tok0:tok0 + 128, :], sl, gmix, "gmix")
            cache_copy_some((2 * NSB + 16 * NP - 1) // (16 * NP), after=[last_])
            if STEP < 2:
                continue
            for half, bank in ((0, pb + 1), (1, pb + 2)):
                for k in range(8):
                    P.op("pe", I.matmul(
                        ps[bank][:, 0:512], lhsT=xT[sl][:, k, :], rhs=win[:, k, half * 512:(half + 1) * 512],
                        start=(k == 0), stop=(k == 7)),
                        r=["xT%d" % sl, WINA[k]], w=["ps%d" % bank])
            if STEP < 3:
                continue
            o, _w = CF["cosP"]
            o2, _w = CF["sinP"]
            rope_qk(sl, cf[:, o + i * 8:o + i * 8 + 8], cf[:, o2 + i * 8:o2 + i * 8 + 8], pb + 1, pb + 2)
            if STEP < 4:
                continue
            P.op("act", I.copy(out=vst[sl][:], in_=ps[pb + 2][:, 256:512]),
                 r=["ps%d" % (pb + 2)], w=["vst%d" % sl])
            P.dma("sp", k_prompt.ap()[tok0:tok0 + 128, :], qk[sl][:, 8:12, :].rearrange("p h d -> p (h d)"),
                  r=["qk%d" % sl])
            P.dma("sp", v_prompt.ap()[tok0:tok0 + 128, :], vst[sl][:], r=["vst%d" % sl], w=["vprompt%d_%d" % (s, i)])
            if STEP < 5:
                continue
            qk_to_bf16(sl)
            if STEP < 6:
                continue
            for cidx in range(6):
                P.op("pe", I.transpose(
                    out=psb[pb + 3][:, cidx * 128:(cidx + 1) * 128],
                    in_=qkb[sl][:, cidx * 128:(cidx + 1) * 128], identity=CBs("ident")),
                    r=["qkb%d" % sl, "cb"], w=["ps%d" % (pb + 3)])
            if STEP < 7:
                continue
            P.op("act", I.copy(out=qT[:, :, i * 128:(i + 1) * 128],
                                         in_=psb[pb + 3][:, 0:512].rearrange("p (c t) -> p c t", c=4)),
                 r=["ps%d" % (pb + 3)], w=["qT%d" % i])
            if STEP < 8:
                continue
            P.op("act", I.copy(out=kT[:, :, i * 128:(i + 1) * 128],
                                                in_=psb[pb + 3][:, 512:768].rearrange("p (c t) -> p c t", c=2)),
                 r=["ps%d" % (pb + 3)], w=["kT%d" % i])
        if stop_after == "p1a":
            break
        for oi, d in enumerate((1, 4, 16)):
            L = S // d
            for r_ in range(d):
                for n in range(L // 128):
                    tile = r_ * (L // 128) + n
                    row0 = s * S + r_ + d * 128 * n
                    src = dap(v_prompt, row0 * 256, [[d * 256, 128], [64, 4], [1, 64]])
                    lo_sub = (r_ + d * 128 * n) // 128
                    hi_sub = (r_ + d * 128 * n + d * 127) // 128
                    P.dma("pool", V3[:, oi, tile, :, 0:64], src, r=["vprompt%d_%d" % (s, ii) for ii in range(lo_sub, hi_sub + 1)],
                          w=["V3_%d_%d" % (oi, tile)])
        unit = 0
        for kv in range(4):
            c, hp = kv // 2, 64 * (kv % 2)
            first = True
            for oi, d in enumerate((1, 4, 16)):
                L = S // d
                nb = L // 128
                for r_ in range(d):
                    for n in range(nb):
                        u = unit % 3
                        unit += 1
                        bS, bO = u, 3 + u
                        t0 = r_ + d * 128 * n
                        tp = t0 - d * 128
                        q_ap = sap(qT, hp, 64, (2 * c) * S + t0, [[S, 2], [d, 128]])
                        kc_ap = sap(kT, hp, 64, c * S + t0, [[d, 128]])
                        ncol = 512 if n > 0 else 256
                        subs_c = sorted(set(range(t0 // 128, (t0 + d * 127) // 128 + 1)))
                        subs_p = sorted(set(range(tp // 128, (tp + d * 127) // 128 + 1))) if n > 0 else []
                        RQ = ["qT%d" % x for x in subs_c]
                        RKC = ["kT%d" % x for x in subs_c]
                        RKP = ["kT%d" % x for x in subs_p]
                        pe_mask = (unit % 2 == 0)
                        P.op("pe", I.matmul(
                            ps[bS][:, 0:256], lhsT=kc_ap, rhs=q_ap, start=True, stop=(not pe_mask)),
                            r=RKC + RQ, w=["ps%d" % bS])
                        if pe_mask:
                            P.op("pe", I.matmul(ps[bS][:, 0:256], lhsT=CBs("ident"), rhs=CBs("mcur"), start=False, stop=True),
                                 r=["cb"], w=["ps%d" % bS])
                        if n > 0:
                            kp_ap = sap(kT, hp, 64, c * S + tp, [[d, 128]])
                            P.op("pe", I.matmul(
                                ps[bS][:, 256:512], lhsT=kp_ap, rhs=q_ap, start=True, stop=(not pe_mask)),
                                r=RKP + RQ, w=["ps%d" % bS])
                            if pe_mask:
                                P.op("pe", I.matmul(ps[bS][:, 256:512], lhsT=CBs("ident"), rhs=CBs("mprev"), start=False, stop=True),
                                     r=["cb"], w=["ps%d" % bS])
                        P.op("act", I.activation(
                            out=ET[u][:, 0:ncol], in_=ps[bS][:, 0:ncol], func=AF.Exp),
                            r=["ps%d" % bS], w=["ET%d" % u])
                        if not pe_mask:
                            P.op("pool", I.tensor_tensor(out=ET[u][:, 0:ncol], in0=ET[u][:, 0:ncol], in1=CBs("m01")[:, 0:ncol], op=ALU.mult),
                                 r=["ET%d" % u, "cb"], w=["ET%d" % u])
                        tile = r_ * nb + n
                        P.op("pe", I.matmul(
                            ps[bO][0:65, 0:256], lhsT=V3[:, oi, tile, kv, :], rhs=ET[u][:, 0:256],
                            start=True, stop=(n == 0)),
                            r=["V3_%d_%d" % (oi, tile), "ET%d" % u], w=["ps%d" % bO])
                        if n > 0:
                            P.op("pe", I.matmul(
                                ps[bO][0:65, 0:256], lhsT=V3[:, oi, tile - 1, kv, :], rhs=ET[u][:, 256:512],
                                start=False, stop=True),
                                r=["V3_%d_%d" % (oi, tile - 1), "ET%d" % u], w=["ps%d" % bO])
                        a_ap = sap(acc, 0, 65, t0, [[S, 2], [d, 128]])
                        o_ap = ps[bO][0:65, 0:256].rearrange("p (g t) -> p g t", g=2)
                        if oi == 0:
                            P.op("act", I.copy(out=a_ap, in_=o_ap),
                                 r=["ps%d" % bO], w=["acc"])
                        else:
                            P.op("dve", I.tensor_tensor(
                                out=a_ap, in0=a_ap, in1=o_ap, op=ALU.add),
                                r=["ps%d" % bO, "acc"], w=["acc"])
            P.op("dve", I.reciprocal(out=acc[64:65, :, :].rearrange("p g t -> p (g t)"),
                                               in_=acc[64:65, :, :].rearrange("p g t -> p (g t)")),
                 r=["acc"], w=["acc"])
            for g in range(2):
                for piece in range(4):
                    bB = 6 + (g * 4 + piece) % 2
                    col0 = g * S + piece * 512
                    P.op("pe", I.matmul(
                        ps[bB][0:64, 0:512], lhsT=CFs("ones", 64, 1)[:, 0:64], rhs=acc[64:65, g, piece * 512:(piece + 1) * 512],
                        start=True, stop=True), r=["cf", "acc"], w=["ps%d" % bB])
                    P.op("dve", I.tensor_tensor(
                        out=mixT[64 * g:64 * g + 64, kv, piece * 512:(piece + 1) * 512],
                        in0=acc[0:64, g, piece * 512:(piece + 1) * 512], in1=ps[bB][0:64, 0:512], op=ALU.mult),
                        r=["ps%d" % bB, "acc"], w=["mixA"])
        if debug_out:
            for kv in range(4):
                P.op("dve", I.tensor_copy(out=acc[:, 0, :], in_=mixT[:, kv, :]), r=["mixA", "acc"], w=["acc"])
                P.dma("sp", dbg.ap()[:, kv * S:(kv + 1) * S], acc[:, 0, :], r=["acc"], w=["dbg"])
        if stop_after == "att":
            break

        P.barrier()
        A2 = alloc_pass2()
        ssd_pass(A2, prompt=True, s=s)
        if stop_after == "p2":
            break
        P.barrier()


    def sample_attention():
        P.barrier()
        arena_reset()
        qk_ = [AT([128, 12, 64], F32)]
        rt_ = [AT([128, 4, 12, 8], F32)]
        vst_ = [AT([128, 256], F32)]
        qkb_ = [AT([128, 768], BF16)]
        qTs = AT([128, 4, 128], BF16)
        kTs = AT([128, 2, 128], BF16)
        Vn = AT([128, 4, 65], BF16)
        Kc = AT([128, 16, 256], BF16)
        Kf = AT([128, 16, 256], F32)
        Vf = AT([128, 16, 256], F32)
        KT = [AT([128, 2, 16, 128], BF16) for k in range(2)]
        Vx = [AT([128, 16, 4, 65], BF16) for k in range(2)]
        Es = [AT([128, 256], BF16) for k in range(2)]
        Ew = [AT([128, 256], BF16) for k in range(2)]
        accs = AT([128, 4, 256], F32)
        nonlocal qk, rt, vst, qkb
        sv = (qk, rt, vst, qkb)
        qk, rt, vst, qkb = qk_, rt_, vst_, qkb_
        rmsnorm_to_xT(x_sample.ap()[0:128, :], 0, gmix, "gmix")
        for half, bank in ((0, 1), (1, 2)):
            for k in range(8):
                P.op("pe", I.matmul(ps[bank][:, 0:512], lhsT=xT[0][:, k, :], rhs=win[:, k, half * 512:(half + 1) * 512],
                                    start=(k == 0), stop=(k == 7)), r=["xT0", WINA[k]], w=["ps%d" % bank])
        oc, _w = CF["cosS"]
        os_, _w = CF["sinS"]
        rope_qk(0, cf[:, oc:oc + 8], cf[:, os_:os_ + 8], 1, 2)
        P.op("act", I.copy(out=vst[0][:], in_=ps[2][:, 256:512]), r=["ps2"], w=["vst0"])
        for b in range(NSB):
            P.dma("sp", k_sample.ap()[b:b + 1, 2040 * 256:S * 256].rearrange("o (t f) -> (o t) f", f=256),
                  qk[0][b * 8:(b + 1) * 8, 8:12, :].rearrange("p h d -> p (h d)"), r=["qk0"])
            P.dma("sp", v_sample.ap()[b:b + 1, 2040 * 256:S * 256].rearrange("o (t f) -> (o t) f", f=256),
                  vst[0][b * 8:(b + 1) * 8, :], r=["vst0"])
        qk_to_bf16(0)
        for cidx in range(6):
            P.op("pe", I.transpose(out=psb[3][:, cidx * 128:(cidx + 1) * 128], in_=qkb[0][:, cidx * 128:(cidx + 1) * 128],
                                   identity=CBs("ident")), r=["qkb0", "cb"], w=["ps3"])
        P.op("act", I.copy(out=qTs[:], in_=psb[3][:, 0:512].rearrange("p (c t) -> p c t", c=4)), r=["ps3"], w=["qTs"])
        P.op("act", I.copy(out=kTs[:], in_=psb[3][:, 512:768].rearrange("p (c t) -> p c t", c=2)), r=["ps3"], w=["kTs"])
        P.op("pool", I.memset(Vn[:].rearrange("p a b -> p (a b)"), 1.0), w=["Vn"])
        P.op("pool", I.tensor_copy(out=Vn[:, :, 0:64], in_=vst[0][:].rearrange("p (a b) -> p a b", a=4)), r=["vst0", "Vn"], w=["Vn"])
        for u in range(2):
            P.op("pool", I.memset(Vx[u][:].rearrange("p a b c -> p (a b c)"), 1.0), w=["Vx%d" % u])
        for kv in range(4):
            c, hp = kv // 2, 64 * (kv % 2)
            u = kv % 2
            q_ap = sap(qTs, hp, 64, 2 * c * 128, [[8, 16], [128, 2], [1, 8]])
            P.op("pe", I.matmul(ps[2 + u][:, 0:256], lhsT=kTs[hp:hp + 64, c, :], rhs=q_ap, start=True, stop=True),
                 r=["kTs", "qTs"], w=["ps%d" % (2 + u)])
            P.op("act", I.activation(out=Es[u][:], in_=ps[2 + u][:, 0:256], func=AF.Exp), r=["ps%d" % (2 + u)], w=["Es%d" % u])
            P.op("dve", I.tensor_tensor(out=Ew[u][:], in0=Es[u][:], in1=CBs("wnew"), op=ALU.mult), r=["Es%d" % u, "cb"], w=["Ew%d" % u])
            bn = 6 + kv // 2
            cn = (kv % 2) * 256
            P.op("pe", I.matmul(ps[bn][0:65, cn:cn + 256], lhsT=Vn[:, kv, :], rhs=Ew[u][:], start=True, stop=True),
                 r=["Vn", "Ew%d" % u], w=["ps%d" % bn])
        P.op("pool", I.memset(Kf[:].rearrange("p a b -> p (a b)"), 0.0), w=["Kf"])
        P.op("pool", I.memset(Vf[:].rearrange("p a b -> p (a b)"), 0.0), w=["Vf"])
        for b in range(NSB):
            ub = b % 2
            for (ct, Xf, tg) in ((cache_k, Kf, "Kf"), (cache_v, Vf, "Vf")):
                P.dma("sp", Xf[:, 0:8, :].rearrange("p a b -> p (a b)"), dap(ct, b * S * 256, [[4096, 128], [1, 2048]]), w=[tg])
                P.dma("sp", Xf[96:128, 8:16, :].rearrange("p a b -> p (a b)"),
                      dap(ct, b * S * 256 + 96 * 4096 + 2048, [[4096, 32], [1, 2048]]), w=[tg])
            P.op("act", I.copy(out=Kc[:].rearrange("p a b -> p (a b)"), in_=Kf[:].rearrange("p a b -> p (a b)")), r=["Kf"], w=["Kc"])
            P.op("dve", I.tensor_copy(out=Vx[ub][:, :, :, 0:64], in_=Vf[:].rearrange("p r (k d) -> p r k d", k=4)),
                 r=["Vf", "Vx%d" % ub], w=["Vx%d" % ub])
            for c in range(2):
                for rh in range(2):
                    bank = 2 + (c * 2 + rh) % 2
                    for rr in range(8):
                        r_ = rh * 8 + rr
                        P.op("pe", I.transpose(out=psb[bank][:, rr * 128:(rr + 1) * 128], in_=Kc[:, r_, c * 128:(c + 1) * 128],
                                               identity=CBs("ident")), r=["Kc", "cb"], w=["ps%d" % bank])
                    P.op("act", I.copy(out=KT[ub][:, c, rh * 8:(rh + 1) * 8, :],
                                       in_=psb[bank][:, 0:1024].rearrange("p (r m) -> p r m", r=8)),
                         r=["ps%d" % bank], w=["KT%d" % ub])
            for kv in range(4):
                c, hp = kv // 2, 64 * (kv % 2)
                u = kv % 2
                q_ap = sap(qTs, hp, 64, 2 * c * 128 + b * 8, [[128, 2], [1, 8]])
                for r_ in range(16):
                    P.op("pe", I.matmul(ps[u][:, r_ * 16:(r_ + 1) * 16], lhsT=KT[ub][hp:hp + 64, c, r_, :], rhs=q_ap,
                                        start=True, stop=True), r=["KT%d" % ub, "qTs"], w=["ps%d" % u])
                P.op("act", I.activation(out=Es[u][:], in_=ps[u][:, 0:256], func=AF.Exp), r=["ps%d" % u], w=["Es%d" % u])
                P.op("dve", I.tensor_tensor(out=Ew[u][:], in0=Es[u][:], in1=CBs("wcache"), op=ALU.mult),
                     r=["Es%d" % u, "cb"], w=["Ew%d" % u])
                bn = 4 + kv // 2
                cn = (kv % 2) * 256 + b * 16
                for r_ in range(16):
                    P.op("pe", I.matmul(ps[bn][0:65, cn:cn + 16], lhsT=Vx[ub][:, r_, kv, :], rhs=Ew[u][:, r_ * 16:(r_ + 1) * 16],
                                        start=(r_ == 0), stop=(r_ == 15)), r=["Vx%d" % ub, "Ew%d" % u], w=["ps%d" % bn])
        for kv in range(4):
            bn, cn = 6 + kv // 2, (kv % 2) * 256
            bc_, cc = 4 + kv // 2, (kv % 2) * 256
            P.op("act", I.copy(out=accs[0:65, kv, :], in_=ps[bn][0:65, cn:cn + 256]), r=["ps%d" % bn], w=["accs%d" % kv])
            P.op("dve", I.tensor_tensor(out=accs[0:65, kv, :], in0=accs[0:65, kv, :], in1=ps[bc_][0:65, cc:cc + 256], op=ALU.add),
                 r=["accs%d" % kv, "ps%d" % bc_], w=["accs%d" % kv])
            P.op("dve", I.reciprocal(out=accs[64:65, kv, :], in_=accs[64:65, kv, :]), r=["accs%d" % kv], w=["accs%d" % kv])
            P.op("pe", I.matmul(ps[kv % 2][0:64, 0:256], lhsT=CFs("ones", 64, 1)[:, 0:64], rhs=accs[64:65, kv, :], start=True, stop=True),
                 r=["cf", "accs%d" % kv], w=["ps%d" % (kv % 2)])
            for g in range(2):
                in0 = sap(accs, 0, 64, kv * 256 + g * 8, [[16, 16], [1, 8]])
                in1 = sap(ps[kv % 2], 0, 64, g * 8, [[16, 16], [1, 8]])
                P.op("dve", I.tensor_tensor(out=mixT[64 * g:64 * g + 64, kv, 0:128].rearrange("p (b t) -> p b t", b=16),
                                            in0=in0, in1=in1, op=ALU.mult),
                     r=["accs%d" % kv, "ps%d" % (kv % 2)], w=["mixA"])
        qk, rt, vst, qkb = sv

    def sample_ssd():
        P.barrier()
        A = alloc_pass2(sample=True)
        A["xT5"] = xT[0]
        pre_s = AT([128, 8, 16, 11], F32)
        stc = AT([128, D], F32)
        h0n = [AT([128, 4, 128], F32) for k in range(2)]
        h0T = [AT([128, 512], BF16) for k in range(2)]
        CTm = AT([128, 2, 16, 128], BF16)
        xwm = [AT([128, 512], BF16) for k in range(2)]
        decq = AT([128, 4, 16], F32)
        abc2 = AT([128, 8, 64], F32)
        stout = [AT([128, 4, 128], F32) for k in range(2)]
        xbcT, cacc, xtok, tm = A["xbcT"][0], A["cacc"], A["xtok"], A["tm"]
        sm, aa, xw = A["sm"][0], A["aa"][0], A["xw"][0]
        P.dma("sp", stc[0:48, :], state_conv.ap()[0:48, :], w=["stc"])
        for ch in range(8):
            P.op("pe", I.transpose(out=ps[4][:, ch * 48:(ch + 1) * 48], in_=stc[0:48, ch * 128:(ch + 1) * 128],
                                   identity=CFs("ident", 0, 48)[:, 0:48]), r=["stc", "cf"], w=["ps4"])
        P.op("act", I.copy(out=pre_s[:, :, :, 0:3], in_=ps[4][:, 0:384].rearrange("p (c b r) -> p c b r", c=8, b=16)),
             r=["ps4"], w=["pre_s"])
        for ch in range(8):
            bank = ch // 4
            for k in range(8):
                P.op("pe", I.matmul(ps[bank][:, (ch % 4) * 128:(ch % 4 + 1) * 128], lhsT=win[:, k, 1536 + ch * 128:1536 + (ch + 1) * 128],
                                    rhs=xT[0][:, k, :], start=(k == 0), stop=(k == 7)), r=["xT0", WIN[k]], w=["ps%d" % bank])
        for bank in range(2):
            P.op("act", I.copy(out=pre_s[:, bank * 4:(bank + 1) * 4, :, 3:11],
                               in_=ps[bank][:, 0:512].rearrange("p (c b t) -> p c b t", c=4, b=16)),
                 r=["ps%d" % bank, "pre_s"], w=["pre_s"])
        for ch in range(8):
            u = ch % 2
            cv = cacc[u][:, 0:128].rearrange("p (b t) -> p b t", b=16)
            P.op("dve", I.tensor_scalar(out=cv, in0=pre_s[:, ch, :, 3:11], scalar1=cw[:, 3, ch:ch + 1], scalar2=cbias[:, ch:ch + 1],
                                        op0=ALU.mult, op1=ALU.add), r=["pre_s", "cw", "cbias"], w=["cacc%d" % u])
            for tap in (2, 1, 0):
                P.op("dve", I.scalar_tensor_tensor(out=cv, in0=pre_s[:, ch, :, tap:tap + 8], scalar=cw[:, tap, ch:ch + 1],
                                                   in1=cv, op0=ALU.mult, op1=ALU.add),
                     r=["pre_s", "cw", "cacc%d" % u], w=["cacc%d" % u])
            P.op("act", I.activation(out=xbcT[:, ch, 0:128], in_=cacc[u][:, 0:128], func=AF.Silu), r=["cacc%d" % u],
                 w=["xbcT0_%d_0" % ch])
        for half in range(2):
            for k in range(8):
                P.op("pe", I.matmul(ps[4 + half][:, 0:512], lhsT=xT[0][:, k, :], rhs=win[:, k, 1536 + half * 512:1536 + (half + 1) * 512],
                                    start=(k == 0), stop=(k == 7)), r=["xT0", WIN[k]], w=["ps%d" % (4 + half)])
            P.op("act", I.copy(out=xtok[:, half * 512:(half + 1) * 512], in_=ps[4 + half][:, 0:512]),
                 r=["ps%d" % (4 + half)], w=["decT_0"])
        for r_ in range(3):
            P.dma("sp", dap(conv_sample, r_ * 1024, [[3 * 1024, 16], [1, 1024]]), sap(xtok, 5 + r_, 1, 0, [[1, 1024]]) if False else
                  bass.AP(xtok.tensor, xtok.offset + (5 + r_) * xtok.ap[0][0], [[8 * xtok.ap[0][0], 16], [1, 1024]]), r=["decT_0"])

        def yoff_emit():
            P.op("dve", I.tensor_copy(out=abc2[:], in_=bc8(aa, 64)), r=["aa_0"], w=["abc2"])
            for q in range(4):
                P.op("pe", I.matmul(ps[3][:, 32 + q * 16:32 + (q + 1) * 16], lhsT=abc2[:, 2 * q:2 * q + 2, :].rearrange("p a b -> p (a b)"),
                                    rhs=CFs("blkcol"), start=True, stop=True), r=["abc2", "cf"], w=["ps3"])
            P.op("act", I.activation(out=decq[:].rearrange("p a b -> p (a b)"), in_=ps[3][:, 32:96], func=AF.Exp), r=["ps3"], w=["decq"])
            for g in range(2):
                xin = bass.AP(xbcT.tensor, xbcT.offset + (6 + g) * 512, [list(xbcT.ap[0]), [0, 16], [1, 128]])
                P.op("dve", I.tensor_tensor(out=CTm[:, g], in0=xin, in1=CBs("blkrow").rearrange("p (b t) -> p b t", b=16), op=ALU.mult),
                     r=["xbcT0_%d_0" % (6 + g), "cb"], w=["CTm"])
            for b in range(NSB):
                ub = b % 2
                P.dma("sp", h0n[ub][:], dap(state_ssm, b * 512 * 128, [[128, 128], [128 * 128, 4], [1, 128]]), w=["h0n%d" % ub])
                bt = 4 + ub
                for q in range(4):
                    P.op("pe", I.transpose(out=ps[bt][:, q * 128:(q + 1) * 128], in_=h0n[ub][:, q, :], identity=CFs("ident")),
                         r=["h0n%d" % ub, "cf"], w=["ps%d" % bt])
                P.op("act", I.copy(out=h0T[ub][:], in_=ps[bt][:, 0:512]), r=["ps%d" % bt], w=["h0T%d" % ub])
                for g in range(2):
                    yo_ap, yo_tag = ((ps[0][:, 0:256], "ps0"), (ps[6][:, 256:512], "ps6"))[g]
                    P.op("pe", I.matmul(yo_ap, lhsT=CTm[:, g, b, :], rhs=h0T[ub][:, g * 256:(g + 1) * 256],
                                        start=(b == 0), stop=(b == NSB - 1)), r=["CTm", "h0T%d" % ub], w=[yo_tag])
                P.op("dve", I.tensor_scalar(out=xwm[ub][:], in0=xw[:], scalar1=CFs("blkcol")[:, b:b + 1], scalar2=None, op0=ALU.mult),
                     r=["xw_0", "cf"], w=["xwm%d" % ub])
                bs = 1 + ub
                for q in range(4):
                    g = q // 2
                    P.op("pe", I.matmul(ps[bs][:, q * 128:(q + 1) * 128], lhsT=xwm[ub][:, q * 128:(q + 1) * 128],
                                        rhs=tm[:, 0, 512 + g * 128:512 + (g + 1) * 128], start=True, stop=True),
                         r=["xwm%d" % ub, "tm0"], w=["ps%d" % bs])
                for q in range(4):
                    P.op("dve", I.scalar_tensor_tensor(out=stout[ub][:, q, :], in0=h0n[ub][:, q, :], scalar=decq[:, q, b:b + 1],
                                                       in1=ps[bs][:, q * 128:(q + 1) * 128], op0=ALU.mult, op1=ALU.add),
                         r=["h0n%d" % ub, "decq", "ps%d" % bs], w=["stout%d" % ub])
                P.dma("sp", dap(ssm_sample, b * 512 * 128, [[128, 128], [128 * 128, 4], [1, 128]]), stout[ub][:], r=["stout%d" % ub])

        ssd_chunk(A, 0, 0, None, True, "triB", "onesB", "dmaskBF", x_sample.ap()[0:128, :], NTOK, False, yoff_emit=yoff_emit)

    cache_copy_some(len(copy_jobs))
    if stop_after is None or stop_after == "sample":
        sample_attention()
        sample_ssd()

    def phase_b(tiles):
        P.barrier()
        arena_pos[0] = 0
        wup = AT([128, 8, 4096], BF16)
        wdn = AT([128, 32, D], BF16)
        gfin = AT([128, D], F32)
        hnT = AT([128, 8, 256], BF16)
        u2T = AT([128, 32, 256], BF16)
        rtmp = [AT([128, 512], F32) for k in range(2)]
        yout = [AT([128, D], F32) for k in range(2)]
        hx = [xs[0], xs[1], xres[0], AT([128, D], F32)]
        ssqB = [T("ssqB%d" % k, [128, 4], F32) for k in range(4)]
        ssqF = [T("ssqF%d" % k, [128, 4], F32) for k in range(2)]
        for k in range(8):
            P.dma("pool", wup[:, k, :], w_up.ap()[k * 128:(k + 1) * 128, :], w=["wup%d" % k])
        for q in range(8):
            P.dma("pool", wdn[:, q * 4:(q + 1) * 4, :], dap(w_down, q * 4 * 128 * D, [[D, 128], [128 * D, 4], [1, D]]),
                  w=["wdn%d" % q])
        P.dma("sp", gmix[:], dap(norm_mlp, 0, [[0, 128], [1, D]]), w=["gmix"])
        P.dma("sp", gfin[:], dap(norm_final, 0, [[0, 128], [1, D]]), w=["gfin"])
        for ti, (row0, nsub, dst_t, dst_row0) in enumerate(tiles):
            ntok = nsub * 128
            par = ti % 2
            for sub in range(nsub):
                hs = 2 * par + sub
                r0 = row0 + sub * 128
                P.dma("sp", hx[hs][:], hbuf.ap()[r0:r0 + 128, :], r=["hbuf_%d" % r0], w=["hx%d" % hs])
                sl = sub
                sqb, sqn = ssqB[hs], "ssqB%d" % hs
                P.op("pool", I.memset(sqb[:], 0.0), w=[sqn])
                P.op("act", I.activation(out=xn[sl][:], in_=hx[hs][:], func=AF.Square, accum_out=sqb[:, 0:1]),
                     r=["hx%d" % hs, sqn], w=["xn%d" % sl, sqn])
                rstd_ops(sqb, sqn, 1.0 / D)
                P.op("dve", I.scalar_tensor_tensor(out=xn[sl][:], in0=hx[hs][:], scalar=sqb[:, 2:3], in1=gmix[:],
                                                   op0=ALU.mult, op1=ALU.mult), r=["hx%d" % hs, sqn, "gmix"], w=["xn%d" % sl])
                pb = 2 + sub
                for k in range(8):
                    P.op("pe", I.transpose(out=psb[pb][:, k * 128:(k + 1) * 128], in_=xn[sl][:, k * 128:(k + 1) * 128],
                                           identity=CBs("ident")), r=["xn%d" % sl, "cb"], w=["ps%d" % pb])
                P.op("act", I.copy(out=hnT[:, :, sub * 128:(sub + 1) * 128], in_=psb[pb][:, 0:1024].rearrange("p (k t) -> p k t", k=8)),
                     r=["ps%d" % pb], w=["hnT%d" % sub])
            HN = ["hnT%d" % x for x in range(nsub)]
            for cp in range(16):
                bank = cp % 4
                u = cp % 2
                for j in range(2):
                    c = 2 * cp + j
                    for k in range(8):
                        P.op("pe", I.matmul(ps[bank][:, j * 256:j * 256 + ntok], lhsT=wup[:, k, c * 128:(c + 1) * 128],
                                            rhs=hnT[:, k, 0:ntok], start=(k == 0), stop=(k == 7)),
                             r=HN + ["wup%d" % k], w=["ps%d" % bank])
                pv = ps[bank][:, 0:512].rearrange("p (j t) -> p j t", j=2)[:, :, 0:ntok]
                rv = rtmp[u][:].rearrange("p (j t) -> p j t", j=2)[:, :, 0:ntok]
                P.op("act", I.activation(out=rv, in_=pv, func=AF.Relu), r=["ps%d" % bank], w=["rtmp%d" % u])
                eng = "dve" if cp % 2 == 0 else "pool"
                P.op(eng, I.tensor_tensor(out=u2T[:, 2 * cp:2 * cp + 2, 0:ntok], in0=rv, in1=rv, op=ALU.mult),
                     r=["rtmp%d" % u], w=["u2T%d" % cp])
            U2 = ["u2T%d" % x for x in range(16)]
            for sub in range(nsub):
                hs = 2 * par + sub
                yo = yout[sub]
                for half in range(2):
                    bank = 4 + 2 * sub + half
                    for c in range(32):
                        P.op("pe", I.matmul(ps[bank][:, 0:512], lhsT=u2T[:, c, sub * 128:(sub + 1) * 128],
                                            rhs=wdn[:, c, half * 512:(half + 1) * 512], start=(c == 0), stop=(c == 31)),
                             r=["u2T%d" % (c // 2), "wdn%d" % (c // 4)], w=["ps%d" % bank])
                    P.op("dve", I.tensor_tensor(out=yo[:, half * 512:(half + 1) * 512], in0=ps[bank][:, 0:512],
                                                in1=hx[hs][:, half * 512:(half + 1) * 512], op=ALU.add),
                         r=["ps%d" % bank, "hx%d" % hs], w=["yout%d" % sub])
                sq, sqn = ssqF[sub], "ssqF%d" % sub
                P.op("pool", I.memset(sq[:], 0.0), w=[sqn])
                P.op("act", I.activation(out=hx[hs][:], in_=yo[:], func=AF.Square, accum_out=sq[:, 0:1]),
                     r=["yout%d" % sub, sqn], w=["hx%d" % hs, sqn])
                rstd_ops(sq, sqn, 1.0 / D)
                P.op("dve", I.scalar_tensor_tensor(out=yo[:], in0=yo[:], scalar=sq[:, 2:3], in1=gfin[:], op0=ALU.mult, op1=ALU.mult),
                     r=["yout%d" % sub, sqn, "gfin"], w=["yout%d" % sub])
                dr = dst_row0 + sub * 128
                P.dma("sp", dst_t.ap()[dr:dr + 128, :], yo[:], r=["yout%d" % sub])

    if stop_after in ("p2",):
        phase_b([(t * 256, 2, y_prompt, t * 256) for t in range(8)])
    elif stop_after == "sample":
        phase_b([(NTOK, 1, y_sample, 0)])
    elif stop_after is None:
        tiles = [(t * 256, 2, y_prompt, t * 256) for t in range(NTOK // 256)]
        tiles.append((NTOK, 1, y_sample, 0))
        phase_b(tiles)

    counts = P.emit()
    st.close()
    return nc, counts


def _core_inputs(inputs, c, cf, cbv):
    f = lambda a: np.ascontiguousarray(a, dtype=np.float32)
    m = {
        "x_prompt": f(inputs["x_prompt"][2 * c:2 * c + 2]).reshape(2 * S, D),
        "x_sample": f(inputs["x_sample"][16 * c:16 * c + 16]).reshape(128, D),
        "cache_k": f(inputs["cache_k"][0, 16 * c:16 * c + 16]).reshape(16, S * 256),
        "cache_v": f(inputs["cache_v"][0, 16 * c:16 * c + 16]).reshape(16, S * 256),
        "state_conv": f(inputs["state_conv"][0, 16 * c:16 * c + 16]).reshape(48, 1024),
        "state_ssm": f(inputs["state_ssm"][0, 16 * c:16 * c + 16]).reshape(16 * 512, 128),
        "w_in": f(inputs["w_in"][0]), "w_out": f(inputs["w_out"][0]),
        "conv_w": f(inputs["conv_w"][0]), "conv_b": f(inputs["conv_b"][0]).reshape(1, 1024),
        "dt_bias": f(inputs["dt_bias"][0]).reshape(1, 8), "a_log": f(inputs["a_log"][0]).reshape(1, 8),
        "d_skip": f(inputs["d_skip"][0]).reshape(1, 8), "ssm_norm": f(inputs["ssm_norm"][0]).reshape(1, 512),
        "norm_mix": f(inputs["norm_mix"][0]).reshape(1, D), "norm_mlp": f(inputs["norm_mlp"][0]).reshape(1, D),
        "w_up": f(inputs["w_up"][0]), "w_down": f(inputs["w_down"][0]),
        "norm_final": f(inputs["norm_final"]).reshape(1, D),
        "cf": cf, "cb": cbv,
    }
    return m


def kernel(**inputs):
    cf, cbv = make_consts()
    nc, _ = build()
    in_maps = [_core_inputs(inputs, c, cf, cbv) for c in range(NCORES)]
    res = run_bass_kernel_spmd(nc, in_maps, core_ids=list(range(NCORES)))
    R = res.results
    cat = lambda name: np.concatenate([np.asarray(r[name]) for r in R], axis=0)
    y_prompt = cat("y_prompt").reshape(16, S, D)
    y_sample = cat("y_sample").reshape(128, 8, D)
    k_prompt = cat("k_prompt").reshape(1, 16, S, 4, 64)
    v_prompt = cat("v_prompt").reshape(1, 16, S, 4, 64)
    conv_prompt = cat("conv_prompt").reshape(1, 16, 3, 1024)
    ssm_prompt = cat("ssm_prompt").reshape(1, 16, 8, 64, 128)
    k_sample = cat("k_sample").reshape(1, 128, S, 4, 64)
    v_sample = cat("v_sample").reshape(1, 128, S, 4, 64)
    conv_sample = cat("conv_sample").reshape(1, 128, 3, 1024)
    ssm_sample = cat("ssm_sample").reshape(1, 128, 8, 64, 128)
    return (y_prompt, y_sample, k_prompt, v_prompt, conv_prompt, ssm_prompt,
            k_sample, v_sample, conv_sample, ssm_sample)
```
